# Optimizing a Trainium2 kernel written in Bass

```python
import math
import jax, jax.numpy as jnp
from jax import lax
import numpy as np

D_MODEL = 1024
BATCH = 8
SEQ = 2048
DEPTH = 4

N_MIXERS = 2
CHUNK = 64
A_HEADS = 8
A_DK = 128
A_DV = 128
A_CONV = 5
A_W = A_HEADS * A_DK
A_VW = A_HEADS * A_DV
A_CONV_CH = 2 * A_W + A_VW
A_IN = 2 * A_W + 2 * A_VW + 4 * A_HEADS
B_HEADS = 4
B_DK = 128
B_DV = 256
B_RANK = 16
B_TAU = 16.0
B_KW = B_HEADS * B_DK
B_VW = B_HEADS * B_DV
B_IN = 2 * B_KW + 2 * B_VW + 2 * B_RANK
D_FF = 4 * D_MODEL
DEEPNORM_ALPHA = (2 * DEPTH) ** 0.25
DEEPNORM_BETA = (8 * DEPTH) ** -0.25
LN_EPS = 1e-5
RMS_EPS = 1e-6
N_A_LAYERS = (DEPTH + 1) // 2
N_B_LAYERS = DEPTH // 2

kernel_name = "hybrid_gdn_gla_deepnorm_encoder"

F32 = jnp.float32


def _layernorm(x, g, b):
    xf = x.astype(F32)
    mu = jnp.mean(xf, axis=-1, keepdims=True)
    xc = xf - mu
    var = jnp.mean(xc * xc, axis=-1, keepdims=True)
    return (xc * lax.rsqrt(var + LN_EPS) * g.astype(F32) + b.astype(F32)).astype(x.dtype)


def _rmsnorm(x, g):
    return x * lax.rsqrt(jnp.mean(x * x, axis=-1, keepdims=True) + RMS_EPS) * g.astype(F32)


def _l2norm(x):
    return x * lax.rsqrt(jnp.sum(x * x, axis=-1, keepdims=True) + 1e-6)


def _flip(t):
    return jnp.flip(t, axis=2)


def _depthwise_conv(u, w):
    k = w.shape[0]
    return lax.conv_general_dilated(
        u, w[:, None, :], window_strides=(1,), padding=[(k // 2, k // 2)],
        dimension_numbers=("NWC", "WIO", "NWC"), feature_group_count=u.shape[-1])


def _gated_delta_chunked(q, k, v, beta, g):
    bn, h, s, dk = q.shape
    dv = v.shape[-1]
    n = s // CHUNK
    q = q.reshape(bn, h, n, CHUNK, dk)
    k = k.reshape(bn, h, n, CHUNK, dk)
    v = v.reshape(bn, h, n, CHUNK, dv)
    beta = beta.reshape(bn, h, n, CHUNK)
    gc = jnp.cumsum(g.reshape(bn, h, n, CHUNK), axis=-1)
    causal = jnp.tril(jnp.ones((CHUNK, CHUNK), bool))
    strict = jnp.tril(jnp.ones((CHUNK, CHUNK), bool), -1)
    decay_mat = jnp.exp(jnp.where(causal, gc[..., :, None] - gc[..., None, :], -jnp.inf))
    kb = k * beta[..., None]
    a_mat = jnp.where(strict, jnp.einsum('bhncd,bhnsd->bhncs', kb, k) * decay_mat, 0.0)
    lhs = a_mat + jnp.eye(CHUNK, dtype=F32)
    rhs = jnp.concatenate([v * beta[..., None], kb * jnp.exp(gc)[..., None]], axis=-1)
    sol = lax.linalg.triangular_solve(lhs, rhs, left_side=True, lower=True, unit_diagonal=True)
    u, w = sol[..., :dv], sol[..., dv:]
    qk = jnp.einsum('bhncd,bhnsd->bhncs', q, k) * decay_mat
    q_dec = q * jnp.exp(gc)[..., None]
    k_dec = k * jnp.exp(gc[..., -1:] - gc)[..., None]
    g_last = jnp.exp(gc[..., -1])
    xs = tuple(jnp.moveaxis(t, 2, 0) for t in (qk, q_dec, k_dec, u, w, g_last))

    def step(state, inp):
        qk_c, qd_c, kd_c, u_c, w_c, gl_c = inp
        v_new = u_c - jnp.einsum('bhcd,bhde->bhce', w_c, state)
        o = jnp.einsum('bhcd,bhde->bhce', qd_c, state) + jnp.einsum('bhcs,bhse->bhce', qk_c, v_new)
        state = state * gl_c[..., None, None] + jnp.einsum('bhcd,bhce->bhde', kd_c, v_new)
        return state, o

    s0 = jnp.zeros((bn, h, dk, dv), F32)
    _, o = lax.scan(step, s0, xs)
    return jnp.moveaxis(o, 0, 2).reshape(bn, h, s, dv)


def _gla_chunked(q, k, v, log_a):
    bn, h, s, dk = q.shape
    dv = v.shape[-1]
    n = s // CHUNK

    def to_chunks(t):
        return jnp.moveaxis(t.reshape(bn, h, n, CHUNK, t.shape[-1]), 2, 0)

    qc, kc, vc = to_chunks(q), to_chunks(k), to_chunks(v)
    bc = jnp.cumsum(to_chunks(log_a), axis=3)
    causal = jnp.tril(jnp.ones((CHUNK, CHUNK), bool))[:, :, None]

    def step(state, inp):
        q_c, k_c, v_c, b_c = inp
        dec = jnp.exp(jnp.where(causal, b_c[:, :, :, None, :] - b_c[:, :, None, :, :], -jnp.inf))
        scores = jnp.einsum('bhid,bhjd,bhijd->bhij', q_c, k_c, dec)
        o = jnp.einsum('bhid,bhde->bhie', q_c * jnp.exp(b_c), state) + jnp.einsum('bhij,bhje->bhie', scores, v_c)
        b_last = b_c[:, :, -1:, :]
        state = jnp.exp(b_last[:, :, 0, :])[..., None] * state + jnp.einsum(
            'bhjd,bhje->bhde', k_c * jnp.exp(b_last - b_c), v_c)
        return state, o

    s0 = jnp.zeros((bn, h, dk, dv), F32)
    _, o = lax.scan(step, s0, (qc, kc, vc, bc))
    return jnp.moveaxis(o, 0, 2).reshape(bn, h, s, dv)


def _mixer_gdn(h, w_in, conv_w, a_log, dt_bias, norm_g, w_out):
    bn, s, _ = h.shape
    proj = jnp.matmul(h, w_in).astype(F32)
    qkv = jax.nn.silu(_depthwise_conv(proj[..., :A_CONV_CH], conv_w.astype(F32)))
    z = proj[..., A_CONV_CH:A_CONV_CH + A_VW]
    ba = proj[..., A_CONV_CH + A_VW:].reshape(bn, s, 2, 2, A_HEADS)

    def heads(t, d):
        return t.reshape(bn, s, A_HEADS, d).transpose(0, 2, 1, 3)

    q = _l2norm(heads(qkv[..., :A_W], A_DK)) * (A_DK ** -0.5)
    k = _l2norm(heads(qkv[..., A_W:2 * A_W], A_DK))
    v = heads(qkv[..., 2 * A_W:], A_DV)
    beta = jax.nn.sigmoid(ba[:, :, 0]).transpose(2, 0, 3, 1)
    g = (-jnp.exp(a_log.astype(F32)) * jax.nn.softplus(ba[:, :, 1] + dt_bias.astype(F32))).transpose(2, 0, 3, 1)
    o_f = _gated_delta_chunked(q, k, v, beta[0], g[0])
    o_b = _flip(_gated_delta_chunked(_flip(q), _flip(k), _flip(v), _flip(beta[1]), _flip(g[1])))
    o = (o_f + o_b).transpose(0, 2, 1, 3)
    o = _rmsnorm(o, norm_g) * jax.nn.silu(z.reshape(bn, s, A_HEADS, A_DV))
    return jnp.matmul(o.reshape(bn, s, A_VW).astype(w_out.dtype), w_out)


def _mixer_gla(h, w_in, gate_w2, gate_b, norm_g, w_out):
    bn, s, _ = h.shape
    proj = jnp.matmul(h, w_in).astype(F32)

    def heads(t, d):
        return t.reshape(bn, s, B_HEADS, d).transpose(0, 2, 1, 3)

    q = heads(proj[..., :B_KW], B_DK) * (B_DK ** -0.5)
    k = heads(proj[..., B_KW:2 * B_KW], B_DK)
    v = heads(proj[..., 2 * B_KW:2 * B_KW + B_VW], B_DV)
    r = proj[..., 2 * B_KW + B_VW:2 * B_KW + 2 * B_VW]
    gl = proj[..., 2 * B_KW + 2 * B_VW:].reshape(bn, s, 2, B_RANK)
    gate_logit = jnp.einsum('bsnr,nrk->nbsk', gl, gate_w2.astype(F32)) + gate_b.astype(F32)[:, None, None, :]
    log_a = (jax.nn.log_sigmoid(gate_logit) / B_TAU).reshape(2, bn, s, B_HEADS, B_DK).transpose(0, 1, 3, 2, 4)
    o_f = _gla_chunked(q, k, v, log_a[0])
    o_b = _flip(_gla_chunked(_flip(q), _flip(k), _flip(v), _flip(log_a[1])))
    o = (o_f + o_b).transpose(0, 2, 1, 3)
    o = _rmsnorm(o, norm_g) * jax.nn.silu(r.reshape(bn, s, B_HEADS, B_DV))
    return jnp.matmul(o.reshape(bn, s, B_VW).astype(w_out.dtype), w_out)


def _sq_relu_mlp(h, w1, w2):
    a = jax.nn.relu(jnp.matmul(h, w1))
    return jnp.matmul(a * a, w2)


def setup_inputs(seed: int = 0) -> dict:
    key = jax.random.key(seed)
    ks = jax.random.split(key, 20)
    nrm = jax.random.normal
    x = nrm(ks[0], (BATCH, SEQ, D_MODEL), F32)
    a_w_in = nrm(ks[1], (N_A_LAYERS, D_MODEL, A_IN), F32) * D_MODEL ** -0.5
    a_conv = nrm(ks[2], (N_A_LAYERS, A_CONV, A_CONV_CH), F32) * A_CONV ** -0.5
    a_alog = jnp.log(jax.random.uniform(ks[3], (N_A_LAYERS, 2, A_HEADS), F32, 1.0, 16.0))
    dt = jnp.exp(jax.random.uniform(ks[4], (N_A_LAYERS, 2, A_HEADS), F32, math.log(1e-3), math.log(1e-1)))
    a_dt_bias = dt + jnp.log(-jnp.expm1(-dt))
    a_norm_g = 1.0 + 0.02 * nrm(ks[5], (N_A_LAYERS, A_DV), F32)
    a_w_out = nrm(ks[6], (N_A_LAYERS, A_VW, D_MODEL), F32) * (A_VW ** -0.5 * DEEPNORM_BETA)
    b_w_in = nrm(ks[7], (N_B_LAYERS, D_MODEL, B_IN), F32) * D_MODEL ** -0.5
    b_gate_w2 = nrm(ks[8], (N_B_LAYERS, 2, B_RANK, B_KW), F32) * B_RANK ** -0.5
    b_gate_b = 0.1 * nrm(ks[9], (N_B_LAYERS, 2, B_KW), F32)
    b_norm_g = 1.0 + 0.02 * nrm(ks[10], (N_B_LAYERS, B_DV), F32)
    b_w_out = nrm(ks[11], (N_B_LAYERS, B_VW, D_MODEL), F32) * (B_VW ** -0.5 * DEEPNORM_BETA)
    ln1_g = 1.0 + 0.02 * nrm(ks[12], (DEPTH, D_MODEL), F32)
    ln1_b = 0.02 * nrm(ks[13], (DEPTH, D_MODEL), F32)
    mlp_w1 = nrm(ks[14], (DEPTH, D_MODEL, D_FF), F32) * D_MODEL ** -0.5
    mlp_w2 = nrm(ks[15], (DEPTH, D_FF, D_MODEL), F32) * (D_FF ** -0.5 * DEEPNORM_BETA)
    ln2_g = 1.0 + 0.02 * nrm(ks[16], (DEPTH, D_MODEL), F32)
    ln2_b = 0.02 * nrm(ks[17], (DEPTH, D_MODEL), F32)
    return {"x": x, "a_w_in": a_w_in, "a_conv": a_conv, "a_alog": a_alog, "a_dt_bias": a_dt_bias,
            "a_norm_g": a_norm_g, "a_w_out": a_w_out, "b_w_in": b_w_in, "b_gate_w2": b_gate_w2,
            "b_gate_b": b_gate_b, "b_norm_g": b_norm_g, "b_w_out": b_w_out, "ln1_g": ln1_g,
            "ln1_b": ln1_b, "mlp_w1": mlp_w1, "mlp_w2": mlp_w2, "ln2_g": ln2_g, "ln2_b": ln2_b}


def reference(x, a_w_in, a_conv, a_alog, a_dt_bias, a_norm_g, a_w_out, b_w_in, b_gate_w2,
              b_gate_b, b_norm_g, b_w_out, ln1_g, ln1_b, mlp_w1, mlp_w2, ln2_g, ln2_b):
    for i in range(DEPTH):
        j = i // N_MIXERS
        if i % N_MIXERS == 0:
            m = _mixer_gdn(x, a_w_in[j], a_conv[j], a_alog[j], a_dt_bias[j], a_norm_g[j], a_w_out[j])
        else:
            m = _mixer_gla(x, b_w_in[j], b_gate_w2[j], b_gate_b[j], b_norm_g[j], b_w_out[j])
        x = _layernorm(DEEPNORM_ALPHA * x + m.astype(x.dtype), ln1_g[i], ln1_b[i])
        x = _layernorm(DEEPNORM_ALPHA * x + _sq_relu_mlp(x, mlp_w1[i], mlp_w2[i]).astype(x.dtype), ln2_g[i], ln2_b[i])
    return x
```

```python
import os
import numpy as np
import concourse.bass as bass
import concourse.mybir as mybir
from concourse.bass_utils import run_bass_kernel_spmd
from contextlib import ExitStack

F32 = mybir.dt.float32
BF16 = mybir.dt.bfloat16
AF = mybir.ActivationFunctionType
ALU = mybir.AluOpType
AX = mybir.AxisListType

D = 1024
SEQ = 2048
NT = 16
KC = 8
DFF = 4096
ALPHA = float(8 ** 0.25)
LN_EPS = 1e-5
BIG = 30000.0


class Sched:
    def __init__(self, nc, es, same_engine_sync=True):
        self.nc = nc
        self.es = es
        self.eng = {"pe": nc.tensor, "act": nc.scalar, "dve": nc.vector, "pool": nc.gpsimd, "sp": nc.sync}
        self.sem = {}
        self.cnt = {}
        self.cur = {}
        self.owner = {}
        self.gen = {}
        for e in self.eng:
            self.sem[e] = es.enter_context(nc.semaphore("s_" + e))
            self.cnt[e] = 0
            self.cur[e] = e
            self.owner[e] = e
            self.gen[e] = 0
            for g in range(1, 4):
                k = "%s#%d" % (e, g)
                self.sem[k] = es.enter_context(nc.semaphore("s_%s_%d" % (e, g)))
                self.owner[k] = e
        self.waited = {e: {} for e in self.eng}
        self.lastw = {}
        self.reads = {}
        self.same = same_engine_sync
        self.nchan = 0
        self.ninst = 0
        self.nwait = 0

    def chan(self, name):
        s = self.es.enter_context(self.nc.semaphore("c_" + name))
        key = "c_%s_%d" % (name, self.nchan)
        self.nchan += 1
        self.sem[key] = s
        self.cnt[key] = 0
        return key

    def _wait(self, e, s, c):
        wd = self.waited[e]
        if wd.get(s, 0) >= c:
            return
        self.eng[e].wait_ge(self.sem[s], c)
        self.nwait += 1
        wd[s] = c

    def _deps(self, e, r, w):
        deps = {}
        for k in r:
            ev = self.lastw.get(k)
            if ev is not None:
                deps[ev[0]] = max(deps.get(ev[0], 0), ev[1])
        for k in w:
            ev = self.lastw.get(k)
            if ev is not None:
                deps[ev[0]] = max(deps.get(ev[0], 0), ev[1])
            for s, c in self.reads.get(k, {}).items():
                deps[s] = max(deps.get(s, 0), c)
        for s, c in deps.items():
            if self.owner.get(s) == e and (e == "pe" or not self.same):
                continue
            self._wait(e, s, c)

    def _commit(self, evs, evc, r, w):
        for k in r:
            d = self.reads.setdefault(k, {})
            d[evs] = max(d.get(evs, 0), evc)
        for k in w:
            self.lastw[k] = (evs, evc)
            self.reads[k] = {}

    def op(self, e, fn, r=(), w=()):
        pk = [k for k in r if isinstance(k, tuple) and k[0] == "ps"]
        if pk:
            r = [k for k in r if not (isinstance(k, tuple) and k[0] == "ps")]
            w = list(w) + pk
        self._deps(e, r, w)
        ins = fn(self.eng[e])
        k = self.cur[e]
        if self.cnt[k] >= 30000:
            self.gen[e] += 1
            k = "%s#%d" % (e, self.gen[e])
            self.cnt[k] = 0
            self.cur[e] = k
            self.owner[k] = e
        self.cnt[k] += 1
        ins.then_inc(self.sem[k], 1)
        self.ninst += 1
        self._commit(k, self.cnt[k], r, w)
        return ins

    def dma(self, q, chan, out, in_, r=(), w=()):
        self._deps(q, r, w)
        ins = self.eng[q].dma_start(out=out, in_=in_)
        self.cnt[chan] += 16
        ins.then_inc(self.sem[chan], 16)
        self.ninst += 1
        self._commit(chan, self.cnt[chan], r, w)
        return ins

    def wait_all(self, e, keys):
        self._deps(e, keys, ())

    def barrier(self):
        for e in self.eng:
            for s, c in self.cnt.items():
                if self.owner.get(s) != e and c > 0:
                    self._wait(e, s, c)


class WStream:
    def __init__(self, S, slots, blocks):
        self.S = S
        self.slots = slots
        self.n = len(slots)
        self.blocks = blocks
        self.issued = 0
        self.chans = [S.chan("ws%d" % i) for i in range(self.n)]
        self.next = 0

    def need(self, tag):
        import os
        if os.environ.get("GLA_STOP") or os.environ.get("GDN_STOP"):
            while self.blocks[self.next][0] != tag:
                self.next += 1
                self.issued = max(self.issued, self.next)
        i = self.next
        assert self.blocks[i][0] == tag, (self.blocks[i][0], tag)
        self.next += 1
        lim = min(len(self.blocks), i + self.n - 1)
        while self.issued < max(lim, i + 1):
            j = self.issued
            sl = j % self.n
            for (o, a) in self.blocks[j][1](self.slots[sl]):
                self.S.dma("pool", self.chans[sl], o, a, w=[("ws", sl)])
            self.issued += 1
        return self.slots[i % self.n], ("ws", i % self.n)


def build(layers=(0, 1, 2, 3), mixer=True):
    nc = bass.Bass("TRN2", target_bir_lowering=False)
    dt = {}

    has_a = any(l % 2 == 0 for l in layers) and mixer
    has_b = any(l % 2 == 1 for l in layers) and mixer

    def din(name, shape):
        if (name.startswith("a_") and not has_a) or (name.startswith("b_") and not has_b):
            return None
        dt[name] = nc.dram_tensor(name, list(shape), F32, kind="ExternalInput").ap()
        return dt[name]

    x_d = din("x", [SEQ, D])
    a_w_in = din("a_w_in", [2, D, 4128])
    a_conv = din("a_conv", [2, 5, 3072])
    a_alog = din("a_alog", [2, 1, 16])
    a_dtb = din("a_dt_bias", [2, 1, 16])
    a_ng = din("a_norm_g", [2, 1, 128])
    a_w_out = din("a_w_out", [2, D, D])
    b_w_in = din("b_w_in", [2, D, 3104])
    b_gw2 = din("b_gate_w2", [2, 2, 16, 512])
    b_gb = din("b_gate_b", [2, 2, 1, 512])
    b_ng = din("b_norm_g", [2, 1, 256])
    b_w_out = din("b_w_out", [2, D, D])
    ln1_g = din("ln1_g", [4, 1, D])
    ln1_b = din("ln1_b", [4, 1, D])
    w1_d = din("mlp_w1", [4, D, DFF])
    w2_d = din("mlp_w2", [4, DFF, D])
    ln2_g = din("ln2_g", [4, 1, D])
    ln2_b = din("ln2_b", [4, 1, D])
    y_d = nc.dram_tensor("y", [SEQ, D], F32, kind="ExternalOutput").ap()

    es = ExitStack()
    with es:
        S = Sched(nc, es)

        uniq = [0]

        def sb(name, shape, dtype, stack=es):
            uniq[0] += 1
            return stack.enter_context(nc.sbuf_tensor("sb%d_%s" % (uniq[0], name), list(shape), dtype))

        x = sb("x", [128, NT, D], F32)
        xT = sb("xT", [128, KC, SEQ], BF16)
        ps = es.enter_context(nc.psum_tensor("ps", [128, 4096], F32))
        ident = sb("ident", [128, 128], F32)
        identb = sb("identb", [128, 128], BF16)
        lnp = sb("lnp", [128, 2, D], F32)
        stats = sb("stats", [128, NT, 2, 6], F32)
        mv = sb("mv", [128, NT, 2], F32)
        rstd = sb("rstd", [128, NT, 2], F32)
        wslots = [sb("wslot%d" % i, [128, 4096], BF16) for i in range(3)]
        c_io = S.chan("io")
        c_par = S.chan("par")

        blocks = []

        def blk(tag, fn):
            blocks.append((tag, fn))

        def plan_mlp(li):
            for tp in range(2):
                for nb in range(8):
                    blk(("w1", li, tp, nb), lambda sl, li=li, nb=nb: [(
                        sl[:, 0:4096].rearrange("p (k n) -> p k n", k=8),
                        w1_d[li][:, nb * 512:(nb + 1) * 512].rearrange("(k p) n -> p k n", p=128))])
                for nh in range(2):
                    for tgp in range(2):
                        for g in range(4):
                            blk(("w2", li, tp, nh, tgp, g), lambda sl, li=li, nh=nh, g=g: [(
                                sl[:, 0:4096].rearrange("p (k n) -> p k n", k=8),
                                w2_d[li][g * 1024:(g + 1) * 1024, nh * 512:(nh + 1) * 512].rearrange("(k p) n -> p k n", p=128))])

        def plan_gdn(li):
            j = li // 2
            blk(("ba", li), lambda sl, j=j: [(sl[:, 0:256].rearrange("p (k n) -> p k n", k=8),
                                              a_w_in[j][:, 4096:4128].rearrange("(k p) n -> p k n", p=128))])
            for h in range(8):
                def fa(sl, j=j, h=h):
                    v = sl[:, 0:4096].rearrange("p (k n) -> p k n", k=8)
                    return [(v[:, :, s_ * 128:(s_ + 1) * 128],
                             a_w_in[j][:, s_ * 1024 + h * 128:s_ * 1024 + (h + 1) * 128].rearrange("(k p) n -> p k n", p=128))
                            for s_ in range(4)]
                blk(("gdnA", li, h), fa)
                blk(("gdnO", li, h), lambda sl, j=j, h=h: [(sl[:, 0:1024], a_w_out[j][h * 128:(h + 1) * 128, :])])

        def plan_gla(li):
            j = li // 2
            blk(("gl", li), lambda sl, j=j: [
                (sl[:, 0:512].rearrange("p (k n) -> p k n", k=8)[:, :, 0:16], b_w_in[j][:, 3072:3088].rearrange("(k p) n -> p k n", p=128)),
                (sl[:, 0:512].rearrange("p (k n) -> p k n", k=8)[:, :, 32:48], b_w_in[j][:, 3088:3104].rearrange("(k p) n -> p k n", p=128))])
            for h in range(4):
                def fa(sl, j=j, h=h):
                    v = sl[:, 0:4096].rearrange("p (k n) -> p k n", k=8)
                    return [(v[:, :, 0:128], b_w_in[j][:, h * 128:(h + 1) * 128].rearrange("(k p) n -> p k n", p=128)),
                            (v[:, :, 128:256], b_w_in[j][:, 512 + h * 128:512 + (h + 1) * 128].rearrange("(k p) n -> p k n", p=128)),
                            (v[:, :, 256:512], b_w_in[j][:, 1024 + h * 256:1024 + (h + 1) * 256].rearrange("(k p) n -> p k n", p=128))]
                blk(("glaA", li, h), fa)
                blk(("glaB", li, h), lambda sl, j=j, h=h: [(sl[:, 0:2048].rearrange("p (k n) -> p k n", k=8),
                                                           b_w_in[j][:, 2048 + h * 256:2048 + (h + 1) * 256].rearrange("(k p) n -> p k n", p=128))])
                blk(("glaO", li, h), lambda sl, j=j, h=h: [(sl[:, 0:2048].rearrange("p (k n) -> p k n", k=2),
                                                           b_w_out[j][h * 256:(h + 1) * 256, :].rearrange("(k p) n -> p k n", p=128))])

        for li in layers:
            if mixer:
                if li % 2 == 0:
                    plan_gdn(li)
                else:
                    plan_gla(li)
            plan_mlp(li)
        WS = WStream(S, wslots, blocks)

        S.op("pool", lambda e: e.memset(ident[:], 1.0), w=["ident"])
        S.op("pool", lambda e: e.affine_select(out=ident[:], in_=ident[:], pattern=[[1, 128]], compare_op=ALU.is_equal,
                                               fill=0.0, base=0, channel_multiplier=-1), r=["ident"], w=["ident"])
        S.op("pool", lambda e: e.tensor_copy(out=identb[:], in_=ident[:]), r=["ident"], w=["identb"])

        ones_f = sb("ones_f", [128, 128], F32)
        onesb = sb("onesb", [1, 512], BF16)
        tri = sb("tri", [128, 4, 128], F32)
        maskT = sb("maskT", [128, 2, 128], BF16)
        S.op("pool", lambda e: e.memset(ones_f[:], 1.0), w=["ones_f"])
        S.op("pool", lambda e: e.memset(onesb[:], 1.0), w=["onesb"])
        for d_ in range(2):
            sgn = 1 if d_ == 0 else -1
            S.op("pool", lambda e: e.affine_select(out=tri[:, 2 * d_, :], in_=ones_f[:], pattern=[[sgn, 128]],
                                                   compare_op=ALU.is_ge, fill=0.0, base=0, channel_multiplier=-sgn),
                 r=["ones_f"], w=["tri"])
            S.op("pool", lambda e: e.tensor_scalar(out=tri[:, 2 * d_ + 1, :], in0=tri[:, 2 * d_, :], scalar1=-1.0, scalar2=None,
                                                   op0=ALU.add), r=["tri"], w=["tri"])
            S.op("pool", lambda e: e.tensor_copy(out=maskT[:, d_, :], in_=tri[:, 2 * d_, :]), r=["tri"], w=["maskT"])

        def mm(out, lhsT, rhs, start, stop, r, w):
            S.op("pe", lambda e: e.matmul(out, lhsT=lhsT, rhs=rhs, start=start, stop=stop), r=r, w=w)

        def transposes(t):
            for half in range(2):
                reg = ps[:, 3072 + half * 512: 3072 + (half + 1) * 512]
                for c in range(4):
                    kc = half * 4 + c
                    S.op("pe", lambda e: e.transpose(reg[:, c * 128:(c + 1) * 128], x[:, t, kc * 128:(kc + 1) * 128], ident[:]),
                         r=[("x", t), "ident"], w=[("ps", 6 + half)])
                S.op("act", lambda e: e.activation(out=xT[:, half * 4:(half + 1) * 4, t * 128:(t + 1) * 128],
                                                   in_=reg.rearrange("p (a b) -> p a b", a=4), func=AF.Copy),
                     r=[("ps", 6 + half)], w=[("xT", t)])

        def load_ln_params(li, which):
            for i, src in enumerate(((ln1_g, ln1_b), (ln2_g, ln2_b))[which]):
                S.dma("sp", c_par, lnp[:, i, :], src[li].partition_broadcast(128), w=[("lnp", i)])

        def layernorm(t, which):
            gk, bk = ("lnp", 0), ("lnp", 1)
            g_bc = lnp[:, 0, :]
            b_bc = lnp[:, 1, :]
            xt = x[:, t, :]
            for c in range(2):
                S.op("dve", lambda e: e.bn_stats(out=stats[:, t, c, :], in_=x[:, t, c * 512:(c + 1) * 512]),
                     r=[("x", t)], w=[("stats", t, c)])
            S.op("dve", lambda e: e.bn_aggr(out=mv[:, t, :], in_=stats[:, t, :, :].rearrange("p a b -> p (a b)")),
                 r=[("stats", t, 0), ("stats", t, 1)], w=[("mv", t)])
            S.op("act", lambda e: e.activation(out=rstd[:, t, 0:1], in_=mv[:, t, 1:2], func=AF.Sqrt, bias=LN_EPS, scale=1.0),
                 r=[("mv", t)], w=[("rstd0", t)])
            S.op("dve", lambda e: e.reciprocal(out=rstd[:, t, 1:2], in_=rstd[:, t, 0:1]), r=[("rstd0", t)], w=[("rstd", t)])
            S.op("dve", lambda e: e.scalar_tensor_tensor(out=xt, in0=xt, scalar=mv[:, t, 0:1], in1=g_bc,
                                                         op0=ALU.subtract, op1=ALU.mult),
                 r=[("x", t), ("mv", t), gk], w=[("x", t)])
            S.op("dve", lambda e: e.scalar_tensor_tensor(out=xt, in0=xt, scalar=rstd[:, t, 1:2], in1=b_bc,
                                                         op0=ALU.mult, op1=ALU.add),
                 r=[("x", t), ("rstd", t), bk], w=[("x", t)])

        def mlp(li, mstack):
            hT = sb("hT", [128, 32, 1024], BF16, mstack)
            rtmp = sb("rtmp", [128, 2, 512], F32, mstack)
            ev = 0
            for tp in range(2):
                tok0 = tp * 1024
                for nb in range(8):
                    wsl, wkey = WS.need(("w1", li, tp, nb))
                    wv = wsl[:, 0:4096].rearrange("p (k n) -> p k n", k=8)
                    for c in range(4):
                        ffc = nb * 4 + c
                        for tg in range(2):
                            bank = 4 + (ev % 2)
                            reg = ps[:, bank * 512:(bank + 1) * 512]
                            for kc in range(KC):
                                mm(reg, wv[:, kc, c * 128:(c + 1) * 128], xT[:, kc, tok0 + tg * 512: tok0 + (tg + 1) * 512],
                                   kc == 0, kc == KC - 1,
                                   r=[wkey] + [("xT", tp * 8 + tg * 4 + i) for i in range(4)], w=[("ps", bank)])
                            rt = rtmp[:, ev % 2, :]
                            S.op("dve", lambda e: e.tensor_scalar(out=rt, in0=reg, scalar1=0.0, scalar2=None, op0=ALU.max),
                                 r=[("ps", bank)], w=[("rtmp", ev % 2)])
                            S.op("act", lambda e: e.activation(out=hT[:, ffc, tg * 512:(tg + 1) * 512], in_=rt, func=AF.Square),
                                 r=[("rtmp", ev % 2)], w=[("hT", ffc, tg)])
                            ev += 1
                for nh in range(2):
                    for tgp in range(2):
                        for g in range(4):
                            wsl, wkey = WS.need(("w2", li, tp, nh, tgp, g))
                            wv = wsl[:, 0:4096].rearrange("p (k n) -> p k n", k=8)
                            for ti in range(4):
                                tl = tgp * 4 + ti
                                reg = ps[:, ti * 512:(ti + 1) * 512]
                                for c in range(8):
                                    ffc = g * 8 + c
                                    mm(reg, hT[:, ffc, tl * 128:(tl + 1) * 128], wv[:, c, :],
                                       g == 0 and c == 0, g == 3 and c == 7,
                                       r=[wkey, ("hT", ffc, tl // 4)], w=[("ps", ti)])
                        for ti in range(4):
                            t = tp * 8 + tgp * 4 + ti
                            reg = ps[:, ti * 512:(ti + 1) * 512]
                            xs = x[:, t, nh * 512:(nh + 1) * 512]
                            S.op("dve", lambda e: e.scalar_tensor_tensor(out=xs, in0=xs, scalar=ALPHA, in1=reg,
                                                                         op0=ALU.mult, op1=ALU.add),
                                 r=[("ps", ti), ("x", t)], w=[("x", t)])
                for tl in range(8):
                    t = tp * 8 + tl
                    layernorm(t, 1)
                    transposes(t)


        def add_residual(t, h, regy):
            for nh in range(2):
                xs = x[:, t, nh * 512:(nh + 1) * 512]
                rg_ = regy[:, nh * 512:(nh + 1) * 512]
                if h == 0:
                    S.op("dve", lambda e: e.scalar_tensor_tensor(out=xs, in0=xs, scalar=ALPHA, in1=rg_, op0=ALU.mult, op1=ALU.add),
                         r=[("ps", nh), ("x", t)], w=[("x", t)])
                else:
                    S.op("dve", lambda e: e.tensor_tensor(out=xs, in0=xs, in1=rg_, op=ALU.add),
                         r=[("ps", nh), ("x", t)], w=[("x", t)])

        def gla(li, ms):
            j = li // 2
            glT = sb("glT", [49, SEQ], BF16, ms)
            gw2b = sb("gw2b", [49, 512], BF16, ms)
            ngbc = sb("ngbc", [128, 256], F32, ms)
            qk = sb("qk", [128, NT, 256], BF16, ms)
            vb = sb("vb", [128, NT, 256], BF16, ms)
            rg = sb("rg", [128, NT, 256], BF16, ms)
            QT = sb("QT", [128, 2, NT, 128], BF16, ms)
            KTt = sb("KTt", [128, 2, 128], BF16, ms)
            scT = sb("scT", [128, 2, NT, 128], BF16, ms)
            kp = sb("kp", [128, 2, NT, 128], BF16, ms)
            Sbs = sb("Sbs", [128, NT, 256], BF16, ms)
            ebl = sb("ebl", [128, 2, NT], F32, ms)
            Sf = sb("Sf", [128, 256], F32, ms)
            Sfb = sb("Sfb", [128, 2, 256], BF16, ms)
            et = sb("et", [128, 1, 128], F32, ms)
            spt = sb("spt", [128, 1, 128], F32, ms)
            e3 = sb("e3", [128, 2, 3, 128], F32, ms)
            qt2 = sb("qt2", [128, 2, 256], BF16, ms)
            rtm = sb("rtm", [128, 1, 256], F32, ms)
            ssq = sb("ssq", [128, NT, 3], F32, ms)
            og = sb("og", [128, 2, 256], BF16, ms)
            ogT = sb("ogT", [128, 2, 2, 128], BF16, ms)
            c_g = S.chan("glap%d" % li)
            S.dma("pool", c_g, gw2b[0:16, :], b_gw2[j][0], w=["gw2b"])
            S.dma("pool", c_g, gw2b[32:48, :], b_gw2[j][1], w=["gw2b"])
            S.dma("pool", c_g, gw2b[16:17, :], b_gb[j][0], w=["gw2b"])
            S.dma("pool", c_g, gw2b[48:49, :], b_gb[j][1], w=["gw2b"])
            S.dma("sp", c_g, ngbc[:], b_ng[j].partition_broadcast(128), w=["ngbc"])
            psb = ps[:, 2560:3072].bitcast(BF16)
            wsl, wkey = WS.need(("gl", li))
            wv = wsl[:, 0:512].rearrange("p (k n) -> p k n", k=8)
            S.op("pool", lambda e: e.memset(wv[:, :, 16:32], 0.0), r=[wkey], w=[wkey])
            for tg in range(4):
                reg = ps[0:48, 1024:1536]
                for kc in range(KC):
                    mm(reg, wv[:, kc, 0:48], xT[:, kc, tg * 512:(tg + 1) * 512], kc == 0, kc == KC - 1,
                       r=[wkey] + [("xT", tg * 4 + i) for i in range(4)], w=[("ps", 2)])
                S.op("act", lambda e: e.activation(out=glT[0:48, tg * 512:(tg + 1) * 512], in_=reg, func=AF.Copy),
                     r=[("ps", 2)], w=[("glT", tg)])
                for rr in (16, 48):
                    S.dma("sp", c_g, glT[rr:rr + 1, tg * 512:(tg + 1) * 512], onesb[0:1, :], r=["onesb"], w=[("glT", tg)])
            import os
            STOP = int(os.environ.get("GLA_STOP", "99"))
            if STOP <= 1:
                return
            for h in range(4):
                if STOP <= 5 and h > 0:
                    return
                wA, kA = WS.need(("glaA", li, h))
                wB, kB = WS.need(("glaB", li, h))
                wAv = wA[:, 0:4096].rearrange("p (k n) -> p k n", k=8)
                wBv = wB[:, 0:2048].rearrange("p (k n) -> p k n", k=8)
                for t in range(NT):
                    reg = ps[:, 1024:1536]
                    for kc in range(KC):
                        mm(reg, xT[:, kc, t * 128:(t + 1) * 128], wAv[:, kc, :], kc == 0, kc == KC - 1,
                           r=[kA, ("xT", t)], w=[("ps", 2)])
                    SK = os.environ.get("GLA_SKIP", "")
                    if "f" not in SK:
                        S.op("act", lambda e: e.activation(out=qk[:, t, :], in_=reg[:, 0:256], func=AF.Copy),
                             r=[("ps", 2)], w=[("qk", t)])
                    if "e" not in SK:
                        S.op("dve", lambda e: e.tensor_copy(out=vb[:, t, :], in_=reg[:, 256:512]), r=[("ps", 2)], w=[("vb", t)])
                    if "B" in SK:
                        continue
                    reg3 = ps[:, 1536:1792]
                    for kc in range(KC):
                        mm(reg3, xT[:, kc, t * 128:(t + 1) * 128], wBv[:, kc, :], kc == 0, kc == KC - 1,
                           r=[kB, ("xT", t)], w=[("ps", 3)])
                    if "c" not in SK:
                        S.op("act", lambda e: e.activation(out=rtm[:, 0, :], in_=reg3, func=AF.Silu),
                             r=[("ps", 3)], w=[("rtm", 0)])
                    if "d" not in SK:
                        S.op("dve", lambda e: e.tensor_tensor(out=rg[:, t, :], in0=rtm[:, 0, :], in1=ngbc[:], op=ALU.mult),
                             r=[("rtm", 0), "ngbc"], w=[("rg", t)])
                if STOP <= 2:
                    return
                it = 0
                for d_ in range(2):
                    for t in range(NT):
                        b_ = it % 2
                        it += 1
                        pg = ps[:, 3072:3200]
                        pc = ps[:, 2176:2432]
                        pcl = ps[:, 2432:2433]
                        mm(pg, glT[32 * d_:32 * d_ + 17, t * 128:(t + 1) * 128], gw2b[32 * d_:32 * d_ + 17, h * 128:(h + 1) * 128], True, True,
                           r=[("glT", t // 4), "gw2b"], w=[("ps", 6)])
                        S.op("act", lambda e: e.activation(out=et[:, 0, :], in_=pg, func=AF.Exp, scale=-1.0),
                             r=[("ps", 6)], w=[("et", 0)])
                        S.op("act", lambda e: e.activation(out=spt[:, 0, :], in_=et[:, 0, :], func=AF.Ln, bias=1.0, scale=1.0),
                             r=[("et", 0)], w=[("spt", 0)])
                        PST = int(os.environ.get("GLA_P", "9"))
                        if PST <= 1:
                            continue
                        mm(pc[:, 0:128], tri[:, 2 * d_, :], spt[:, 0, :], True, True, r=["tri", ("spt", 0)], w=[("ps", 4)])
                        mm(pc[:, 128:256], tri[:, 2 * d_ + 1, :], spt[:, 0, :], True, True, r=["tri", ("spt", 0)], w=[("ps", 4)])
                        mm(pcl, spt[:, 0, :], ones_f[:, 0:1], True, True, r=["ones_f", ("spt", 0)], w=[("ps", 4)])
                        S.op("act", lambda e: e.activation(out=e3[:, b_, 0, :], in_=pc[:, 0:128], func=AF.Exp, scale=-1.0 / 16),
                             r=[("ps", 4)], w=[("e3", b_, 0)])
                        S.op("act", lambda e: e.activation(out=e3[:, b_, 1, :], in_=pc[:, 0:128], func=AF.Exp, scale=1.0 / 16),
                             r=[("ps", 4)], w=[("e3", b_, 1)])
                        S.op("act", lambda e: e.activation(out=e3[:, b_, 2, :], in_=pc[:, 128:256], func=AF.Exp, scale=1.0 / 16),
                             r=[("ps", 4)], w=[("e3", b_, 2)])
                        S.op("act", lambda e: e.activation(out=ebl[:, d_, t:t + 1], in_=pcl, func=AF.Exp, scale=-1.0 / 16),
                             r=[("ps", 4)], w=[("ebl", d_, t)])
                        if PST <= 2:
                            continue
                        S.op("dve", lambda e: e.scalar_tensor_tensor(out=qt2[:, b_, 0:128], in0=qk[:, t, 0:128], scalar=float(128 ** -0.5),
                                                                     in1=e3[:, b_, 0, :], op0=ALU.mult, op1=ALU.mult),
                             r=[("qk", t), ("e3", b_, 0)], w=[("qt2", b_)])
                        S.op("dve", lambda e: e.tensor_tensor(out=qt2[:, b_, 128:256], in0=qk[:, t, 128:256], in1=e3[:, b_, 1, :], op=ALU.mult),
                             r=[("qk", t), ("e3", b_, 1)], w=[("qt2", b_)])
                        S.op("dve", lambda e: e.tensor_tensor(out=kp[:, d_, t, :], in0=qk[:, t, 128:256], in1=e3[:, b_, 2, :], op=ALU.mult),
                             r=[("qk", t), ("e3", b_, 2)], w=[("kp", d_, t)])
                        if PST <= 3:
                            continue
                        for c in range(2):
                            S.op("pe", lambda e: e.transpose(psb[:, c * 128:(c + 1) * 128], qt2[:, b_, c * 128:(c + 1) * 128], identb[:]),
                                 r=[("qt2", b_), "identb"], w=[("ps", 5)])
                        S.op("act", lambda e: e.activation(out=QT[:, d_, t, :], in_=psb[:, 0:128], func=AF.Copy),
                             r=[("ps", 5)], w=[("QT", d_, t)])
                        S.op("act", lambda e: e.activation(out=KTt[:, b_, :], in_=psb[:, 128:256], func=AF.Copy),
                             r=[("ps", 5)], w=[("KTt", b_)])
                        if PST <= 4:
                            continue
                        psc = ps[:, 3584:3712]
                        mm(psc, KTt[:, b_, :], QT[:, d_, t, :], True, True, r=[("QT", d_, t), ("KTt", b_)], w=[("ps", 7)])
                        S.op("dve", lambda e: e.tensor_tensor(out=scT[:, d_, t, :], in0=psc, in1=maskT[:, d_, :], op=ALU.mult),
                             r=[("ps", 7), "maskT"], w=[("scT", d_, t)])
                if STOP <= 3:
                    return
                pS = ps[:, 1024:1280]
                S.op("pool", lambda e: e.memset(Sf[:], 0.0), w=["Sf"])
                for t in range(NT - 1, -1, -1):
                    S.op("act", lambda e: e.activation(out=Sbs[:, t, :], in_=Sf[:], func=AF.Copy), r=["Sf"], w=[("Sbs", t)])
                    mm(pS, kp[:, 1, t, :], vb[:, t, :], True, True, r=[("kp", 1, t), ("vb", t)], w=[("ps", 2)])
                    S.op("dve", lambda e: e.scalar_tensor_tensor(out=Sf[:], in0=Sf[:], scalar=ebl[:, 1, t:t + 1], in1=pS,
                                                                 op0=ALU.mult, op1=ALU.add),
                         r=["Sf", ("ebl", 1, t), ("ps", 2)], w=["Sf"])
                if STOP <= 4:
                    return
                wO, kO = WS.need(("glaO", li, h))
                wOv = wO[:, 0:2048].rearrange("p (k n) -> p k n", k=2)
                S.op("pool", lambda e: e.memset(Sf[:], 0.0), r=["Sf"], w=["Sf"])
                for t in range(NT):
                    b_ = t % 2
                    S.op("act", lambda e: e.activation(out=Sfb[:, b_, :], in_=Sf[:], func=AF.Copy), r=["Sf"], w=[("Sfb", b_)])
                    po = ps[:, 1536:1792]
                    mm(po, QT[:, 0, t, :], Sfb[:, b_, :], True, False, r=[("QT", 0, t), ("Sfb", b_)], w=[("ps", 3)])
                    mm(po, scT[:, 0, t, :], vb[:, t, :], False, False, r=[("scT", 0, t), ("vb", t)], w=[("ps", 3)])
                    mm(po, QT[:, 1, t, :], Sbs[:, t, :], False, False, r=[("QT", 1, t), ("Sbs", t)], w=[("ps", 3)])
                    mm(po, scT[:, 1, t, :], vb[:, t, :], False, True, r=[("scT", 1, t), ("vb", t)], w=[("ps", 3)])
                    mm(pS, kp[:, 0, t, :], vb[:, t, :], True, True, r=[("kp", 0, t), ("vb", t)], w=[("ps", 2)])
                    S.op("dve", lambda e: e.scalar_tensor_tensor(out=Sf[:], in0=Sf[:], scalar=ebl[:, 0, t:t + 1], in1=pS,
                                                                 op0=ALU.mult, op1=ALU.add),
                         r=["Sf", ("ebl", 0, t), ("ps", 2)], w=["Sf"])
                    S.op("act", lambda e: e.activation(out=e3[:, 0, 0:2, :].rearrange("p a b -> p (a b)"), in_=po, func=AF.Square, accum_out=ssq[:, t, 0:1]),
                         r=[("ps", 3)], w=[("e3", 0, 0), ("e3", 0, 1), ("ssq", t)])
                    S.op("act", lambda e: e.activation(out=ssq[:, t, 1:2], in_=ssq[:, t, 0:1], func=AF.Sqrt, bias=1e-6, scale=1.0 / 256),
                         r=[("ssq", t)], w=[("ssq1", t)])
                    S.op("dve", lambda e: e.reciprocal(out=ssq[:, t, 2:3], in_=ssq[:, t, 1:2]), r=[("ssq1", t)], w=[("ssq2", t)])
                    S.op("dve", lambda e: e.scalar_tensor_tensor(out=og[:, b_, :], in0=po, scalar=ssq[:, t, 2:3], in1=rg[:, t, :],
                                                                 op0=ALU.mult, op1=ALU.mult),
                         r=[("ps", 3), ("ssq2", t), ("rg", t)], w=[("og", b_)])
                    for c in range(2):
                        S.op("pe", lambda e: e.transpose(psb[:, c * 128:(c + 1) * 128], og[:, b_, c * 128:(c + 1) * 128], identb[:]),
                             r=[("og", b_), "identb"], w=[("ps", 5)])
                    S.op("act", lambda e: e.activation(out=ogT[:, b_, :, :], in_=psb[:, 0:256].rearrange("p (a b) -> p a b", a=2), func=AF.Copy),
                         r=[("ps", 5)], w=[("ogT", b_)])
                    regy = ps[:, 0:1024]
                    for nh in range(2):
                        for c in range(2):
                            mm(regy[:, nh * 512:(nh + 1) * 512], ogT[:, b_, c, :], wOv[:, c, nh * 512:(nh + 1) * 512], c == 0, c == 1,
                               r=[kO, ("ogT", b_)], w=[("ps", nh)])
                    add_residual(t, h, regy)
                    if h == 3:
                        layernorm(t, 0)
                        transposes(t)


        def gdn(li, ms):
            j = li // 2
            ba = sb("ba", [128, NT, 32], F32, ms)
            sc = sb("gsc", [128, 9, NT, 16], F32, ms)
            alb = sb("alb", [128, 3, 16], F32, ms)
            maskM = sb("maskM", [128, 2, 256], BF16, ms)
            ngb = sb("ngb", [128, 128], F32, ms)
            uT = sb("uT", [128, 2052], BF16, ms)
            cwn = sb("cwn", [5, 128], F32, ms)
            cw = sb("cw", [128, 5], F32, ms)
            cdiag = sb("cdiag", [128, 5, 128], BF16, ms)
            qkn = sb("qkn", [128, NT, 256], BF16, ms)
            vb = sb("vba", [128, NT, 128], BF16, ms)
            qkT = sb("qkT", [128, NT, 256], BF16, ms)
            zg = sb("zg", [128, NT, 128], BF16, ms)
            oacc = sb("oacc", [128, NT, 128], F32, ms)
            cs = sb("cs", [128, 2, 128], F32, ms)
            sqt = sb("sqt", [128, 128], F32, ms)
            nrm = sb("nrm", [128, NT, 3, 3], F32, ms)
            zt = sb("zt", [128, 2, 128], F32, ms)
            diag2 = sb("diag2", [128, 2, 256], F32, ms)
            Dm = sb("Dm", [128, 2, 2, 128], F32, ms)
            Mb = sb("Mb", [128, 2, 2, 128], F32, ms)
            MTT = sb("MTT", [128, 2, 2, 256], F32, ms)
            TTb = sb("TTb", [128, 2, 128], BF16, ms)
            QKTb = sb("QKTb", [128, 2, 128], BF16, ms)
            kbg = sb("kbg", [128, 2, 128], BF16, ms)
            vbeta = sb("vbeta", [128, 2, 128], BF16, ms)
            WT = sb("WT", [128, 2, 128], BF16, ms)
            Ub = sb("Ub", [128, 2, 128], F32, ms)
            dg = sb("dg", [128, 2, 128], BF16, ms)
            QdT = sb("QdT", [128, 2, 128], BF16, ms)
            Kd = sb("Kd", [128, 2, 128], BF16, ms)
            vnew = sb("vnew", [128, 2, 128], BF16, ms)
            Sst = sb("Sst", [128, 2, 128], F32, ms)
            Sbb = sb("Sbb", [128, 2, 128], BF16, ms)
            ogb = sb("ogb", [128, 128], BF16, ms)
            ogTb = sb("ogTb", [128, 128], BF16, ms)
            fin = sb("fin", [128, NT, 3], F32, ms)
            c_g = S.chan("gdnp%d" % li)
            GST = int(os.environ.get("GDN_STOP", "99"))
            S.dma("sp", c_g, alb[:, 0, :], a_alog[j].partition_broadcast(128), w=["alb0"])
            S.dma("sp", c_g, alb[:, 1, :], a_dtb[j].partition_broadcast(128), w=["alb1"])
            S.dma("sp", c_g, ngb[:], a_ng[j].partition_broadcast(128), w=["ngb"])
            for d_ in range(2):
                S.op("pool", lambda e: e.memset(maskM[:, d_, 0:128], -BIG), w=[("maskM", d_)])
                S.op("pool", lambda e: e.memset(maskM[:, d_, 128:256], BIG), r=[("maskM", d_)], w=[("maskM", d_)])
                sg = 1 if d_ == 0 else -1
                S.op("pool", lambda e: e.affine_select(out=maskM[:, d_, 0:128], in_=maskM[:, d_, 0:128], pattern=[[sg, 128]],
                                                       compare_op=ALU.is_ge, fill=0.0, base=0, channel_multiplier=-sg),
                     r=[("maskM", d_)], w=[("maskM", d_)])
                S.op("pool", lambda e: e.affine_select(out=maskM[:, d_, 128:256], in_=maskM[:, d_, 128:256], pattern=[[-sg, 128]],
                                                       compare_op=ALU.is_ge, fill=0.0, base=-1, channel_multiplier=sg),
                     r=[("maskM", d_)], w=[("maskM", d_)])
            S.op("pool", lambda e: e.memset(uT[:, 0:2], 0.0), w=["uTpad"])
            S.op("pool", lambda e: e.memset(uT[:, 2050:2052], 0.0), r=["uTpad"], w=["uTpad"])
            wsl, wkey = WS.need(("ba", li))
            wv = wsl[:, 0:256].rearrange("p (k n) -> p k n", k=8)
            reg = ps[:, 1024:1536]
            for t in range(NT):
                for kc in range(KC):
                    mm(reg[:, t * 32:(t + 1) * 32], xT[:, kc, t * 128:(t + 1) * 128], wv[:, kc, :], kc == 0, kc == KC - 1,
                       r=[wkey, ("xT", t)], w=[("ps", 2)])
            S.op("act", lambda e: e.activation(out=ba[:].rearrange("p a b -> p (a b)"), in_=reg, func=AF.Copy), r=[("ps", 2)], w=["ba"])
            bc16 = lambda ap: ap.unsqueeze(1).to_broadcast([128, NT, 16])
            S.op("act", lambda e: e.activation(out=sc[:, 0], in_=ba[:, :, 0:16], func=AF.Sigmoid), r=["ba"], w=[("sc", 0)])
            S.op("dve", lambda e: e.tensor_scalar(out=sc[:, 1], in0=sc[:, 0], scalar1=-1.0, scalar2=None, op0=ALU.mult),
                 r=[("sc", 0)], w=[("sc", 1)])
            S.op("act", lambda e: e.activation(out=alb[:, 2, :], in_=alb[:, 0, :], func=AF.Exp), r=["alb0"], w=["alb2"])
            S.op("dve", lambda e: e.tensor_tensor(out=sc[:, 8], in0=ba[:, :, 16:32], in1=bc16(alb[:, 1, :]), op=ALU.add),
                 r=["ba", "alb1"], w=[("sc", 8)])
            S.op("act", lambda e: e.activation(out=sc[:, 8], in_=sc[:, 8], func=AF.Exp), r=[("sc", 8)], w=[("sc", 8)])
            S.op("act", lambda e: e.activation(out=sc[:, 8], in_=sc[:, 8], func=AF.Ln, bias=1.0, scale=1.0), r=[("sc", 8)], w=[("sc", 8)])
            S.op("dve", lambda e: e.tensor_tensor(out=sc[:, 8], in0=sc[:, 8], in1=bc16(alb[:, 2, :]), op=ALU.mult),
                 r=[("sc", 8), "alb2"], w=[("sc", 8)])
            pcs = ps[:, 1024:1280]
            pct = ps[:, 1280:1536]
            mm(pcs[:, 0:128], tri[:, 0, :], sc[:, 8, :, 0:8], True, True, r=["tri", ("sc", 8)], w=[("ps", 2)])
            mm(pcs[:, 128:256], tri[:, 2, :], sc[:, 8, :, 8:16], True, True, r=["tri", ("sc", 8)], w=[("ps", 2)])
            mm(pct, ones_f[:], sc[:, 8].rearrange("p a b -> p (a b)"), True, True, r=["ones_f", ("sc", 8)], w=[("ps", 2)])
            for d_ in range(2):
                S.op("act", lambda e: e.activation(out=sc[:, 2, :, d_ * 8:(d_ + 1) * 8],
                                                   in_=pcs[:, d_ * 128:(d_ + 1) * 128].rearrange("p (a b) -> p a b", a=NT), func=AF.Copy),
                     r=[("ps", 2)], w=[("sc", 2)])
            pct3 = pct.rearrange("p (a b) -> p a b", a=NT)
            S.op("dve", lambda e: e.tensor_scalar(out=sc[:, 3], in0=sc[:, 2], scalar1=-1.0, scalar2=None, op0=ALU.mult),
                 r=[("sc", 2)], w=[("sc", 3)])
            S.op("act", lambda e: e.activation(out=sc[:, 4], in_=sc[:, 2], func=AF.Exp, scale=-1.0), r=[("sc", 2)], w=[("sc", 4)])
            S.op("dve", lambda e: e.tensor_tensor(out=sc[:, 5], in0=sc[:, 2], in1=pct3, op=ALU.subtract),
                 r=[("sc", 2), ("ps", 2)], w=[("sc", 5)])
            S.op("act", lambda e: e.activation(out=sc[:, 5], in_=sc[:, 5], func=AF.Exp), r=[("sc", 5)], w=[("sc", 5)])
            S.op("act", lambda e: e.activation(out=sc[:, 7], in_=pct3, func=AF.Exp, scale=-1.0), r=[("ps", 2)], w=[("sc", 7)])
            S.op("dve", lambda e: e.tensor_tensor(out=sc[:, 6], in0=sc[:, 0], in1=sc[:, 4], op=ALU.mult),
                 r=[("sc", 0), ("sc", 4)], w=[("sc", 6)])
            SCK = [("sc", i) for i in range(8)]
            if GST <= 1:
                return
            psb3 = ps[:, 1536:2048].bitcast(BF16)
            for h in range(8):
                if GST <= 7 and h > 0:
                    return
                wA, kA = WS.need(("gdnA", li, h))
                wAv = wA[:, 0:4096].rearrange("p (k n) -> p k n", k=8)
                for t in range(NT):
                    rz = ps[:, 1536 + (t % 2) * 128:1536 + (t % 2 + 1) * 128]
                    for kc in range(KC):
                        mm(rz, xT[:, kc, t * 128:(t + 1) * 128], wAv[:, kc, 384:512], kc == 0, kc == KC - 1,
                           r=[kA, ("xT", t)], w=[("ps", 3)])
                    S.op("act", lambda e: e.activation(out=zt[:, t % 2, :], in_=rz, func=AF.Silu), r=[("ps", 3)], w=[("zt", t % 2)])
                    S.op("dve", lambda e: e.tensor_tensor(out=zg[:, t, :], in0=zt[:, t % 2, :], in1=ngb[:], op=ALU.mult),
                         r=[("zt", t % 2), "ngb"], w=[("zg", t)])
                for c in range(3):
                    S.dma("sp", c_g, cwn[:], a_conv[j][:, c * 1024 + h * 128:c * 1024 + (h + 1) * 128], w=["cwn"])
                    S.op("pe", lambda e: e.transpose(ps[:, 1536 + 256:1536 + 261], cwn[0:5, :], ident[0:5, 0:5]), r=["cwn", "ident"], w=[("ps", 3)])
                    S.op("act", lambda e: e.activation(out=cw[:], in_=ps[:, 1536 + 256:1536 + 261], func=AF.Copy), r=[("ps", 3)], w=["cw"])
                    for jt in range(5):
                        S.op("dve", lambda e: e.tensor_scalar(out=cdiag[:, jt, :], in0=ident[:], scalar1=cw[:, jt:jt + 1], scalar2=None, op0=ALU.mult),
                             r=["ident", "cw"], w=[("cdiag", jt)])
                    for tg in range(4):
                        reg = ps[:, 1024:1536]
                        for kc in range(KC):
                            mm(reg, wAv[:, kc, c * 128:(c + 1) * 128], xT[:, kc, tg * 512:(tg + 1) * 512], kc == 0, kc == KC - 1,
                               r=[kA] + [("xT", tg * 4 + i) for i in range(4)], w=[("ps", 2)])
                        S.op("act", lambda e: e.activation(out=uT[:, 2 + tg * 512:2 + (tg + 1) * 512], in_=reg, func=AF.Copy),
                             r=[("ps", 2)], w=[("uT", tg)])
                    for t in range(NT):
                        b_ = t % 2
                        rc = ps[:, 1536 + b_ * 128:1536 + (b_ + 1) * 128]
                        ukeys = ["uTpad"] + [("uT", g_) for g_ in {max(0, (t * 128 - 2)) // 512, min(2047, t * 128 + 130) // 512}]
                        for jt in range(5):
                            mm(rc, uT[:, t * 128 + jt:t * 128 + jt + 128], cdiag[:, jt, :], jt == 0, jt == 4,
                               r=ukeys + [("cdiag", jt)], w=[("ps", 3)])
                        if c == 2:
                            S.op("act", lambda e: e.activation(out=vb[:, t, :], in_=rc, func=AF.Silu), r=[("ps", 3)], w=[("vb", t)])
                            continue
                        S.op("act", lambda e: e.activation(out=cs[:, b_, :], in_=rc, func=AF.Silu), r=[("ps", 3)], w=[("cs", b_)])
                        S.op("dve", lambda e: e.tensor_tensor(out=sqt[:], in0=cs[:, b_, :], in1=cs[:, b_, :], op=ALU.mult),
                             r=[("cs", b_)], w=["sqt"])
                        S.op("dve", lambda e: e.tensor_reduce(out=nrm[:, t, c, 0:1], in_=sqt[:], axis=AX.X, op=ALU.add),
                             r=["sqt"], w=[("nrm", t, c)])
                        S.op("act", lambda e: e.activation(out=nrm[:, t, c, 1:2], in_=nrm[:, t, c, 0:1], func=AF.Sqrt, bias=1e-6, scale=1.0),
                             r=[("nrm", t, c)], w=[("nrm1", t, c)])
                        S.op("dve", lambda e: e.reciprocal(out=nrm[:, t, c, 2:3], in_=nrm[:, t, c, 1:2]), r=[("nrm1", t, c)], w=[("nrm2", t, c)])
                        qsc = float(128 ** -0.5) if c == 0 else 1.0
                        S.op("dve", lambda e: e.tensor_scalar(out=qkn[:, t, c * 128:(c + 1) * 128], in0=cs[:, b_, :], scalar1=nrm[:, t, c, 2:3],
                                                              scalar2=qsc, op0=ALU.mult, op1=ALU.mult),
                             r=[("cs", b_), ("nrm2", t, c)], w=[("qkn", t, c)])
                for t in range(NT):
                    S.op("pe", lambda e: e.transpose(psb3[:, 0:128], qkn[:, t, 128:256], identb[:]), r=[("qkn", t, 1), "identb"], w=[("ps", 3)])
                    S.op("pe", lambda e: e.transpose(psb3[:, 128:256], qkn[:, t, 0:128], identb[:]), r=[("qkn", t, 0), "identb"], w=[("ps", 3)])
                    S.op("act", lambda e: e.activation(out=qkT[:, t, :], in_=psb3[:, 0:256], func=AF.Copy), r=[("ps", 3)], w=[("qkT", t)])
                if GST <= 2:
                    return
                wO, kO = WS.need(("gdnO", li, h))
                for d_ in range(2):
                    S.op("pool", lambda e: e.memset(Sst[:, d_, :], 0.0), r=[("Sst", d_)], w=[("Sst", d_)])
                    S.op("pool", lambda e: e.memset(Sbb[:, d_, :], 0.0), r=[("Sbb", d_)], w=[("Sbb", d_)])
                for s in range(NT):
                    tt_ = (s, NT - 1 - s)
                    hd_ = (h, 8 + h)
                    bA = (4, 6)
                    bB = (5, 7)

                    def scal(i, d_):
                        return sc[:, i, tt_[d_], hd_[d_]:hd_[d_] + 1]

                    def pA(d_, a, b):
                        return ps[:, bA[d_] * 512 + a:bA[d_] * 512 + b]

                    def pB(d_, a, b):
                        return ps[:, bB[d_] * 512 + a:bB[d_] * 512 + b]

                    def st1(d_):
                        t = tt_[d_]
                        S.op("dve", lambda e: e.tensor_scalar(out=diag2[:, d_, :].rearrange("p (a b) -> p a b", a=2),
                                                              in0=ident[:].unsqueeze(1).to_broadcast([128, 2, 128]),
                                                              scalar1=scal(2, d_), scalar2=None, op0=ALU.mult),
                             r=["ident"] + SCK, w=[("diag2", d_)])
                        mm(pA(d_, 0, 256), ones_f[:], diag2[:, d_, :], True, False, r=["ones_f", ("diag2", d_)], w=[("ps", bA[d_])])
                        mm(pA(d_, 0, 256), identb[:], maskM[:, d_, :], False, True, r=["identb", ("maskM", d_)], w=[("ps", bA[d_])])
                        S.op("act", lambda e: e.activation(out=Dm[:, d_, 0, :], in_=pA(d_, 0, 128), func=AF.Exp, bias=scal(3, d_), scale=1.0),
                             r=[("ps", bA[d_])] + SCK, w=[("Dm", d_, 0)])
                        S.op("act", lambda e: e.activation(out=Dm[:, d_, 1, :], in_=pA(d_, 128, 256), func=AF.Exp, bias=scal(2, d_), scale=-1.0),
                             r=[("ps", bA[d_])] + SCK, w=[("Dm", d_, 1)])
                        mm(pA(d_, 256, 512), qkT[:, t, 0:128], qkT[:, t, :], True, True, r=[("qkT", t)], w=[("ps", bA[d_])])
                        S.op("dve", lambda e: e.scalar_tensor_tensor(out=Mb[:, d_, 0, :], in0=pA(d_, 256, 384), scalar=scal(1, d_), in1=Dm[:, d_, 0, :],
                                                                     op0=ALU.mult, op1=ALU.mult),
                             r=[("ps", bA[d_]), ("Dm", d_, 0)] + SCK, w=[("Mb", d_, 0)])
                        S.op("dve", lambda e: e.tensor_tensor(out=QKTb[:, d_, :], in0=pA(d_, 384, 512), in1=Dm[:, d_, 1, :], op=ALU.mult),
                             r=[("ps", bA[d_]), ("Dm", d_, 1)], w=[("QKTb", d_)])
                        S.op("pe", lambda e: e.transpose(pB(d_, 384, 512), Mb[:, d_, 0, :], ident[:]), r=[("Mb", d_, 0), "ident"], w=[("ps", bB[d_])])
                        S.op("act", lambda e: e.activation(out=MTT[:, d_, 0, 0:128], in_=pB(d_, 384, 512), func=AF.Copy),
                             r=[("ps", bB[d_])], w=[("MT", d_, 0)])
                        S.op("dve", lambda e: e.tensor_tensor(out=MTT[:, d_, 1, 128:256], in0=MTT[:, d_, 0, 0:128], in1=ident[:], op=ALU.add),
                             r=[("MT", d_, 0), "ident"], w=[("TT", d_, 1)])
                        S.op("dve", lambda e: e.tensor_scalar(out=kbg[:, d_, :], in0=qkn[:, t, 128:256], scalar1=scal(6, d_), scalar2=None, op0=ALU.mult),
                             r=[("qkn", t, 1)] + SCK, w=[("kbg", d_)])
                        S.op("dve", lambda e: e.tensor_scalar(out=vbeta[:, d_, :], in0=vb[:, t, :], scalar1=scal(0, d_), scalar2=None, op0=ALU.mult),
                             r=[("vb", t)] + SCK, w=[("vbeta", d_)])
                        S.op("dve", lambda e: e.tensor_scalar(out=dg[:, d_, :], in0=ident[:], scalar1=scal(4, d_), scalar2=None, op0=ALU.mult),
                             r=["ident"] + SCK, w=[("dg", d_)])
                        S.op("dve", lambda e: e.tensor_scalar(out=Kd[:, d_, :], in0=qkn[:, t, 128:256], scalar1=scal(5, d_), scalar2=None, op0=ALU.mult),
                             r=[("qkn", t, 1)] + SCK, w=[("Kd", d_)])

                    def lvl(k):
                        def f(d_):
                            a, b = k % 2, (k + 1) % 2
                            if k <= 5:
                                mm(pB(d_, 0, 128), MTT[:, d_, a, 0:128], Mb[:, d_, a, :], True, True,
                                   r=[("MT", d_, a), ("Mb", d_, a)], w=[("ps", bB[d_])])
                            if k == 0:
                                mm(pB(d_, 128, 256), Mb[:, d_, a, :], MTT[:, d_, a, 0:128], True, True,
                                   r=[("MT", d_, a), ("Mb", d_, a)], w=[("ps", bB[d_])])
                            elif k <= 4:
                                mm(pB(d_, 128, 384), Mb[:, d_, a, :], MTT[:, d_, a, :], True, True,
                                   r=[("MT", d_, a), ("TT", d_, a), ("Mb", d_, a)], w=[("ps", bB[d_])])
                            else:
                                mm(pB(d_, 256, 384), Mb[:, d_, a, :], MTT[:, d_, a, 128:256], True, True,
                                   r=[("TT", d_, a), ("Mb", d_, a)], w=[("ps", bB[d_])])
                            if k <= 5:
                                S.op("act", lambda e: e.activation(out=Mb[:, d_, b, :], in_=pB(d_, 0, 128), func=AF.Copy),
                                     r=[("ps", bB[d_])], w=[("Mb", d_, b)])
                            if k <= 4:
                                S.op("act", lambda e: e.activation(out=MTT[:, d_, b, 0:128], in_=pB(d_, 128, 256), func=AF.Copy),
                                     r=[("ps", bB[d_])], w=[("MT", d_, b)])
                            if k == 0:
                                S.op("dve", lambda e: e.tensor_copy(out=MTT[:, d_, b, 128:256], in_=MTT[:, d_, b, 128:256]),
                                     r=[("TT", d_, b)], w=[("TT", d_, b)])
                            elif k <= 5:
                                S.op("dve", lambda e: e.tensor_tensor(out=MTT[:, d_, b, 128:256], in0=pB(d_, 256, 384), in1=MTT[:, d_, a, 128:256], op=ALU.add),
                                     r=[("ps", bB[d_]), ("TT", d_, a)], w=[("TT", d_, b)])
                            else:
                                S.op("dve", lambda e: e.tensor_tensor(out=TTb[:, d_, :], in0=pB(d_, 256, 384), in1=MTT[:, d_, a, 128:256], op=ALU.add),
                                     r=[("ps", bB[d_]), ("TT", d_, a)], w=[("TTb", d_)])
                        return f

                    def st3(d_):
                        t = tt_[d_]
                        mm(pA(d_, 0, 128), kbg[:, d_, :], TTb[:, d_, :], True, True, r=[("kbg", d_), ("TTb", d_)], w=[("ps", bA[d_])])
                        mm(pA(d_, 128, 256), TTb[:, d_, :], vbeta[:, d_, :], True, True, r=[("vbeta", d_), ("TTb", d_)], w=[("ps", bA[d_])])
                        mm(pA(d_, 256, 384), qkn[:, t, 0:128], dg[:, d_, :], True, True, r=[("qkn", t, 0), ("dg", d_)], w=[("ps", bA[d_])])
                        S.op("act", lambda e: e.activation(out=WT[:, d_, :], in_=pA(d_, 0, 128), func=AF.Copy), r=[("ps", bA[d_])], w=[("WT", d_)])
                        S.op("act", lambda e: e.activation(out=Ub[:, d_, :], in_=pA(d_, 128, 256), func=AF.Copy), r=[("ps", bA[d_])], w=[("Ub", d_)])
                        S.op("act", lambda e: e.activation(out=QdT[:, d_, :], in_=pA(d_, 256, 384), func=AF.Copy), r=[("ps", bA[d_])], w=[("QdT", d_)])

                    def st4(d_):
                        t = tt_[d_]
                        mm(pB(d_, 0, 128), WT[:, d_, :], Sbb[:, d_, :], True, True, r=[("WT", d_), ("Sbb", d_)], w=[("ps", bB[d_])])
                        S.op("dve", lambda e: e.tensor_tensor(out=vnew[:, d_, :], in0=Ub[:, d_, :], in1=pB(d_, 0, 128), op=ALU.subtract),
                             r=[("Ub", d_), ("ps", bB[d_])], w=[("vnew", d_)])
                        mm(pB(d_, 128, 256), QdT[:, d_, :], Sbb[:, d_, :], True, False, r=[("QdT", d_), ("Sbb", d_)], w=[("ps", bB[d_])])
                        mm(pB(d_, 128, 256), QKTb[:, d_, :], vnew[:, d_, :], False, True, r=[("QKTb", d_), ("vnew", d_)], w=[("ps", bB[d_])])
                        mm(pB(d_, 256, 384), Kd[:, d_, :], vnew[:, d_, :], True, True, r=[("Kd", d_), ("vnew", d_)], w=[("ps", bB[d_])])
                        if s <= 7:
                            S.op("act", lambda e: e.activation(out=oacc[:, t, :], in_=pB(d_, 128, 256), func=AF.Copy), r=[("ps", bB[d_])], w=[("oacc", t)])
                        else:
                            S.op("dve", lambda e: e.tensor_tensor(out=oacc[:, t, :], in0=oacc[:, t, :], in1=pB(d_, 128, 256), op=ALU.add),
                                 r=[("ps", bB[d_]), ("oacc", t)], w=[("oacc", t)])
                        S.op("dve", lambda e: e.scalar_tensor_tensor(out=Sst[:, d_, :], in0=Sst[:, d_, :], scalar=scal(7, d_), in1=pB(d_, 256, 384),
                                                                     op0=ALU.mult, op1=ALU.add),
                             r=[("Sst", d_), ("ps", bB[d_])] + SCK, w=[("Sst", d_)])
                        S.op("act", lambda e: e.activation(out=Sbb[:, d_, :], in_=Sst[:, d_, :], func=AF.Copy), r=[("Sst", d_)], w=[("Sbb", d_)])

                    for stage in [st1] + [lvl(k) for k in range(7)] + [st3, st4]:
                        for d_ in range(2):
                            stage(d_)
                    if s >= 8:
                        for d_ in range(2):
                            t = tt_[d_]
                            S.op("act", lambda e: e.activation(out=sqt[:], in_=oacc[:, t, :], func=AF.Square, accum_out=fin[:, t, 0:1]),
                                 r=[("oacc", t)], w=["sqt", ("fin", t)])
                            S.op("act", lambda e: e.activation(out=fin[:, t, 1:2], in_=fin[:, t, 0:1], func=AF.Sqrt, bias=1e-6, scale=1.0 / 128),
                                 r=[("fin", t)], w=[("fin1", t)])
                            S.op("dve", lambda e: e.reciprocal(out=fin[:, t, 2:3], in_=fin[:, t, 1:2]), r=[("fin1", t)], w=[("fin2", t)])
                            S.op("dve", lambda e: e.scalar_tensor_tensor(out=ogb[:], in0=oacc[:, t, :], scalar=fin[:, t, 2:3], in1=zg[:, t, :],
                                                                         op0=ALU.mult, op1=ALU.mult),
                                 r=[("oacc", t), ("fin2", t), ("zg", t)], w=["ogb"])
                            S.op("pe", lambda e: e.transpose(psb3[:, 512:640], ogb[:], identb[:]), r=["ogb", "identb"], w=[("ps", 3)])
                            S.op("act", lambda e: e.activation(out=ogTb[:], in_=psb3[:, 512:640], func=AF.Copy), r=[("ps", 3)], w=["ogTb"])
                            regy = ps[:, 0:1024]
                            for nh in range(2):
                                mm(regy[:, nh * 512:(nh + 1) * 512], ogTb[:], wO[:, nh * 512:(nh + 1) * 512], True, True,
                                   r=[kO, "ogTb"], w=[("ps", nh)])
                            add_residual(t, h, regy)
                            if h == 7:
                                layernorm(t, 0)
                                transposes(t)

        S.dma("sp", c_io, x[:], x_d.rearrange("(t p) d -> p t d", p=128), w=[("x", t) for t in range(NT)])
        for t in range(NT):
            transposes(t)
        for li in layers:
            load_ln_params(li, 0)
            if mixer:
                with ExitStack() as ms:
                    if li % 2 == 0:
                        gdn(li, ms)
                    else:
                        gla(li, ms)
                    S.barrier()
            else:
                for t in range(NT):
                    S.op("dve", lambda e: e.tensor_scalar(out=x[:, t, :], in0=x[:, t, :], scalar1=ALPHA, scalar2=None, op0=ALU.mult),
                         r=[("x", t)], w=[("x", t)])
                    layernorm(t, 0)
                    transposes(t)
            S.barrier()
            load_ln_params(li, 1)
            with ExitStack() as mstack:
                mlp(li, mstack)
                S.barrier()
        S.dma("sp", c_io, y_d.rearrange("(t p) d -> p t d", p=128), x[:], r=[("x", t) for t in range(NT)], w=["y"])
        S.wait_all("sp", ["y"])
        print("ninst", S.ninst, "nwait", S.nwait)
    nc.used_inputs = set(dt.keys())
    return nc


_SHAPES = {
    "a_alog": (2, 1, 16), "a_dt_bias": (2, 1, 16), "a_norm_g": (2, 1, 128), "b_gate_b": (2, 2, 1, 512),
    "b_norm_g": (2, 1, 256), "ln1_g": (4, 1, D), "ln1_b": (4, 1, D), "ln2_g": (4, 1, D), "ln2_b": (4, 1, D),
}


def make_in_maps(inputs, n_cores=8):
    shared = {}
    for k, v in inputs.items():
        if k == "x":
            continue
        a = np.ascontiguousarray(np.asarray(v, dtype=np.float32))
        if k in _SHAPES:
            a = a.reshape(_SHAPES[k])
        shared[k] = a
    xs = np.asarray(inputs["x"], dtype=np.float32)
    maps = []
    for c in range(n_cores):
        m = dict(shared)
        m["x"] = np.ascontiguousarray(xs[c])
        maps.append(m)
    return maps


FUSED = False


def kernel(**inputs):
    maps = make_in_maps(inputs, 8)
    groups = [(0, 1, 2, 3)] if FUSED else [(0,), (1,), (2,), (3,)]
    for g in groups:
        nc = build(layers=g)
        ms = [{k: v for k, v in m.items() if k in nc.used_inputs} for m in maps]
        res = run_bass_kernel_spmd(nc, ms, core_ids=list(range(8)))
        for c in range(8):
            maps[c]["x"] = np.ascontiguousarray(res.results[c]["y"])
    return np.stack([maps[c]["x"] for c in range(8)], axis=0).astype(np.float32)
```

```python
import os
import numpy as np
import concourse.bass as bass
import concourse.mybir as mybir
from concourse.bass_utils import run_bass_kernel_spmd
from contextlib import ExitStack

F32 = mybir.dt.float32
BF16 = mybir.dt.bfloat16
AF = mybir.ActivationFunctionType
ALU = mybir.AluOpType
AX = mybir.AxisListType

D = 1024
SEQ = 2048
NT = 16
KC = 8
DFF = 4096
ALPHA = float(8 ** 0.25)
LN_EPS = 1e-5
BIG = 30000.0


class Sched:
    def __init__(self, nc, es, same_engine_sync=True):
        self.nc = nc
        self.es = es
        self.eng = {"pe": nc.tensor, "act": nc.scalar, "dve": nc.vector, "pool": nc.gpsimd, "sp": nc.sync}
        self.sem = {}
        self.cnt = {}
        self.cur = {}
        self.owner = {}
        self.gen = {}
        for e in self.eng:
            self.sem[e] = es.enter_context(nc.semaphore("s_" + e))
            self.cnt[e] = 0
            self.cur[e] = e
            self.owner[e] = e
            self.gen[e] = 0
            for g in range(1, 8):
                k = "%s#%d" % (e, g)
                self.sem[k] = es.enter_context(nc.semaphore("s_%s_%d" % (e, g)))
                self.owner[k] = e
        self.waited = {e: {} for e in self.eng}
        self.lastw = {}
        self.reads = {}
        self.same = same_engine_sync
        self.nchan = 0
        self.ninst = 0
        self.nwait = 0

    def chan(self, name):
        s = self.es.enter_context(self.nc.semaphore("c_" + name))
        key = "c_%s_%d" % (name, self.nchan)
        self.nchan += 1
        self.sem[key] = s
        self.cnt[key] = 0
        return key

    def _wait(self, e, s, c):
        wd = self.waited[e]
        if wd.get(s, 0) >= c:
            return
        self.eng[e].wait_ge(self.sem[s], c)
        self.nwait += 1
        wd[s] = c

    def _deps(self, e, r, w):
        deps = {}
        for k in r:
            ev = self.lastw.get(k)
            if ev is not None:
                deps[ev[0]] = max(deps.get(ev[0], 0), ev[1])
        for k in w:
            ev = self.lastw.get(k)
            if ev is not None:
                deps[ev[0]] = max(deps.get(ev[0], 0), ev[1])
            for s, c in self.reads.get(k, {}).items():
                deps[s] = max(deps.get(s, 0), c)
        for s, c in deps.items():
            if self.owner.get(s) == e and (e == "pe" or not self.same):
                continue
            self._wait(e, s, c)

    def _commit(self, evs, evc, r, w):
        for k in r:
            d = self.reads.setdefault(k, {})
            d[evs] = max(d.get(evs, 0), evc)
        for k in w:
            self.lastw[k] = (evs, evc)
            self.reads[k] = {}

    def op(self, e, fn, r=(), w=()):
        pk = [k for k in r if isinstance(k, tuple) and k[0] == "ps"]
        if pk:
            r = [k for k in r if not (isinstance(k, tuple) and k[0] == "ps")]
            w = list(w) + pk
        self._deps(e, r, w)
        ins = fn(self.eng[e])
        k = self.cur[e]
        if self.cnt[k] >= int(os.environ.get("ROLL", "30000")):
            self.gen[e] += 1
            k = "%s#%d" % (e, self.gen[e])
            self.cnt[k] = 0
            self.cur[e] = k
            self.owner[k] = e
        self.cnt[k] += 1
        ins.then_inc(self.sem[k], 1)
        self.ninst += 1
        self._commit(k, self.cnt[k], r, w)
        return ins

    def dma(self, q, chan, out, in_, r=(), w=()):
        self._deps(q, r, w)
        ins = self.eng[q].dma_start(out=out, in_=in_)
        self.cnt[chan] += 16
        ins.then_inc(self.sem[chan], 16)
        self.ninst += 1
        self._commit(chan, self.cnt[chan], r, w)
        return ins

    def wait_all(self, e, keys):
        self._deps(e, keys, ())

    def barrier(self):
        for e in self.eng:
            for s, c in self.cnt.items():
                if self.owner.get(s) != e and c > 0:
                    self._wait(e, s, c)


class WStream:
    def __init__(self, S, slots, blocks):
        self.S = S
        self.slots = slots
        self.n = len(slots)
        self.blocks = blocks
        self.issued = 0
        self.chans = [S.chan("ws%d" % i) for i in range(self.n)]
        self.next = 0

    def need(self, tag):
        import os
        if os.environ.get("GLA_STOP") or os.environ.get("GDN_STOP"):
            while self.blocks[self.next][0] != tag:
                self.next += 1
                self.issued = max(self.issued, self.next)
        i = self.next
        assert self.blocks[i][0] == tag, (self.blocks[i][0], tag)
        self.next += 1
        lim = min(len(self.blocks), i + self.n - 1)
        while self.issued < max(lim, i + 1):
            j = self.issued
            sl = j % self.n
            for (o, a) in self.blocks[j][1](self.slots[sl]):
                self.S.dma("pool", self.chans[sl], o, a, w=[("ws", sl)])
            self.issued += 1
        return self.slots[i % self.n], ("ws", i % self.n)


def build(layers=(0, 1, 2, 3), mixer=True):
    nc = bass.Bass("TRN2", target_bir_lowering=False)
    dt = {}

    has_a = any(l % 2 == 0 for l in layers) and mixer
    has_b = any(l % 2 == 1 for l in layers) and mixer

    def din(name, shape):
        if (name.startswith("a_") and not has_a) or (name.startswith("b_") and not has_b):
            return None
        dt[name] = nc.dram_tensor(name, list(shape), F32, kind="ExternalInput").ap()
        return dt[name]

    x_d = din("x", [SEQ, D])
    a_w_in = din("a_w_in", [2, D, 4128])
    a_conv = din("a_conv", [2, 5, 3072])
    a_alog = din("a_alog", [2, 1, 16])
    a_dtb = din("a_dt_bias", [2, 1, 16])
    a_ng = din("a_norm_g", [2, 1, 128])
    a_w_out = din("a_w_out", [2, D, D])
    b_w_in = din("b_w_in", [2, D, 3104])
    b_gw2 = din("b_gate_w2", [2, 2, 16, 512])
    b_gb = din("b_gate_b", [2, 2, 1, 512])
    b_ng = din("b_norm_g", [2, 1, 256])
    b_w_out = din("b_w_out", [2, D, D])
    ln1_g = din("ln1_g", [4, 1, D])
    ln1_b = din("ln1_b", [4, 1, D])
    w1_d = din("mlp_w1", [4, D, DFF])
    w2_d = din("mlp_w2", [4, DFF, D])
    ln2_g = din("ln2_g", [4, 1, D])
    ln2_b = din("ln2_b", [4, 1, D])
    y_d = nc.dram_tensor("y", [SEQ, D], F32, kind="ExternalOutput").ap()

    es = ExitStack()
    with es:
        S = Sched(nc, es)

        uniq = [0]

        def sb(name, shape, dtype, stack=es):
            uniq[0] += 1
            return stack.enter_context(nc.sbuf_tensor("sb%d_%s" % (uniq[0], name), list(shape), dtype))

        x = sb("x", [128, NT, D], F32)
        xT = sb("xT", [128, KC, SEQ], BF16)
        ps = es.enter_context(nc.psum_tensor("ps", [128, 4096], F32))
        ident = sb("ident", [128, 128], F32)
        identb = sb("identb", [128, 128], BF16)
        lnp = sb("lnp", [128, 2, D], F32)
        stats = sb("stats", [128, NT, 2, 6], F32)
        mv = sb("mv", [128, NT, 2], F32)
        rstd = sb("rstd", [128, NT, 2], F32)
        wslots = [sb("wslot%d" % i, [128, 4096], BF16) for i in range(3)]
        c_io = S.chan("io")
        c_par = S.chan("par")

        blocks = []

        def blk(tag, fn):
            blocks.append((tag, fn))

        def plan_mlp(li):
            for tp in range(2):
                for nb in range(8):
                    blk(("w1", li, tp, nb), lambda sl, li=li, nb=nb: [(
                        sl[:, 0:4096].rearrange("p (k n) -> p k n", k=8),
                        w1_d[li][:, nb * 512:(nb + 1) * 512].rearrange("(k p) n -> p k n", p=128))])
                for nh in range(2):
                    for tgp in range(2):
                        for g in range(4):
                            blk(("w2", li, tp, nh, tgp, g), lambda sl, li=li, nh=nh, g=g: [(
                                sl[:, 0:4096].rearrange("p (k n) -> p k n", k=8),
                                w2_d[li][g * 1024:(g + 1) * 1024, nh * 512:(nh + 1) * 512].rearrange("(k p) n -> p k n", p=128))])

        def plan_gdn(li):
            j = li // 2
            blk(("ba", li), lambda sl, j=j: [(sl[:, 0:256].rearrange("p (k n) -> p k n", k=8),
                                              a_w_in[j][:, 4096:4128].rearrange("(k p) n -> p k n", p=128))])
            for h in range(8):
                def fa(sl, j=j, h=h):
                    v = sl[:, 0:4096].rearrange("p (k n) -> p k n", k=8)
                    return [(v[:, :, s_ * 128:(s_ + 1) * 128],
                             a_w_in[j][:, s_ * 1024 + h * 128:s_ * 1024 + (h + 1) * 128].rearrange("(k p) n -> p k n", p=128))
                            for s_ in range(4)]
                blk(("gdnA", li, h), fa)
                blk(("gdnO", li, h), lambda sl, j=j, h=h: [(sl[:, 0:1024], a_w_out[j][h * 128:(h + 1) * 128, :])])

        def plan_gla(li):
            j = li // 2
            blk(("gl", li), lambda sl, j=j: [
                (sl[:, 0:512].rearrange("p (k n) -> p k n", k=8)[:, :, 0:16], b_w_in[j][:, 3072:3088].rearrange("(k p) n -> p k n", p=128)),
                (sl[:, 0:512].rearrange("p (k n) -> p k n", k=8)[:, :, 32:48], b_w_in[j][:, 3088:3104].rearrange("(k p) n -> p k n", p=128))])
            for h in range(4):
                def fa(sl, j=j, h=h):
                    v = sl[:, 0:4096].rearrange("p (k n) -> p k n", k=8)
                    return [(v[:, :, 0:128], b_w_in[j][:, h * 128:(h + 1) * 128].rearrange("(k p) n -> p k n", p=128)),
                            (v[:, :, 128:256], b_w_in[j][:, 512 + h * 128:512 + (h + 1) * 128].rearrange("(k p) n -> p k n", p=128)),
                            (v[:, :, 256:512], b_w_in[j][:, 1024 + h * 256:1024 + (h + 1) * 256].rearrange("(k p) n -> p k n", p=128))]
                blk(("glaA", li, h), fa)
                blk(("glaB", li, h), lambda sl, j=j, h=h: [(sl[:, 0:2048].rearrange("p (k n) -> p k n", k=8),
                                                           b_w_in[j][:, 2048 + h * 256:2048 + (h + 1) * 256].rearrange("(k p) n -> p k n", p=128))])
                blk(("glaO", li, h), lambda sl, j=j, h=h: [(sl[:, 0:2048].rearrange("p (k n) -> p k n", k=2),
                                                           b_w_out[j][h * 256:(h + 1) * 256, :].rearrange("(k p) n -> p k n", p=128))])

        for li in layers:
            if mixer:
                if li % 2 == 0:
                    plan_gdn(li)
                else:
                    plan_gla(li)
            plan_mlp(li)
        WS = WStream(S, wslots, blocks)

        S.op("pool", lambda e: e.memset(ident[:], 1.0), w=["ident"])
        S.op("pool", lambda e: e.affine_select(out=ident[:], in_=ident[:], pattern=[[1, 128]], compare_op=ALU.is_equal,
                                               fill=0.0, base=0, channel_multiplier=-1), r=["ident"], w=["ident"])
        S.op("pool", lambda e: e.tensor_copy(out=identb[:], in_=ident[:]), r=["ident"], w=["identb"])

        ones_f = sb("ones_f", [128, 128], F32)
        onesb = sb("onesb", [1, 512], BF16)
        tri = sb("tri", [128, 4, 128], F32)
        maskT = sb("maskT", [128, 2, 128], BF16)
        S.op("pool", lambda e: e.memset(ones_f[:], 1.0), w=["ones_f"])
        S.op("pool", lambda e: e.memset(onesb[:], 1.0), w=["onesb"])
        for d_ in range(2):
            sgn = 1 if d_ == 0 else -1
            S.op("pool", lambda e: e.affine_select(out=tri[:, 2 * d_, :], in_=ones_f[:], pattern=[[sgn, 128]],
                                                   compare_op=ALU.is_ge, fill=0.0, base=0, channel_multiplier=-sgn),
                 r=["ones_f"], w=["tri"])
            S.op("pool", lambda e: e.tensor_scalar(out=tri[:, 2 * d_ + 1, :], in0=tri[:, 2 * d_, :], scalar1=-1.0, scalar2=None,
                                                   op0=ALU.add), r=["tri"], w=["tri"])
            S.op("pool", lambda e: e.tensor_copy(out=maskT[:, d_, :], in_=tri[:, 2 * d_, :]), r=["tri"], w=["maskT"])

        S.barrier()
        def mm(out, lhsT, rhs, start, stop, r, w):
            S.op("pe", lambda e: e.matmul(out, lhsT=lhsT, rhs=rhs, start=start, stop=stop), r=r, w=w)

        def transposes(t):
            for half in range(2):
                reg = ps[:, 3072 + half * 512: 3072 + (half + 1) * 512]
                for c in range(4):
                    kc = half * 4 + c
                    S.op("pe", lambda e: e.transpose(reg[:, c * 128:(c + 1) * 128], x[:, t, kc * 128:(kc + 1) * 128], ident[:]),
                         r=[("x", t), "ident"], w=[("ps", 6 + half)])
                S.op("act", lambda e: e.activation(out=xT[:, half * 4:(half + 1) * 4, t * 128:(t + 1) * 128],
                                                   in_=reg.rearrange("p (a b) -> p a b", a=4), func=AF.Copy),
                     r=[("ps", 6 + half)], w=[("xT", t)])

        def load_ln_params(li, which):
            for i, src in enumerate(((ln1_g, ln1_b), (ln2_g, ln2_b))[which]):
                S.dma("sp", c_par, lnp[:, i, :], src[li].partition_broadcast(128), w=[("lnp", i)])

        def layernorm(t, which):
            gk, bk = ("lnp", 0), ("lnp", 1)
            g_bc = lnp[:, 0, :]
            b_bc = lnp[:, 1, :]
            xt = x[:, t, :]
            for c in range(2):
                S.op("dve", lambda e: e.bn_stats(out=stats[:, t, c, :], in_=x[:, t, c * 512:(c + 1) * 512]),
                     r=[("x", t)], w=[("stats", t, c)])
            S.op("dve", lambda e: e.bn_aggr(out=mv[:, t, :], in_=stats[:, t, :, :].rearrange("p a b -> p (a b)")),
                 r=[("stats", t, 0), ("stats", t, 1)], w=[("mv", t)])
            S.op("act", lambda e: e.activation(out=rstd[:, t, 0:1], in_=mv[:, t, 1:2], func=AF.Sqrt, bias=LN_EPS, scale=1.0),
                 r=[("mv", t)], w=[("rstd0", t)])
            S.op("dve", lambda e: e.reciprocal(out=rstd[:, t, 1:2], in_=rstd[:, t, 0:1]), r=[("rstd0", t)], w=[("rstd", t)])
            S.op("dve", lambda e: e.scalar_tensor_tensor(out=xt, in0=xt, scalar=mv[:, t, 0:1], in1=g_bc,
                                                         op0=ALU.subtract, op1=ALU.mult),
                 r=[("x", t), ("mv", t), gk], w=[("x", t)])
            S.op("dve", lambda e: e.scalar_tensor_tensor(out=xt, in0=xt, scalar=rstd[:, t, 1:2], in1=b_bc,
                                                         op0=ALU.mult, op1=ALU.add),
                 r=[("x", t), ("rstd", t), bk], w=[("x", t)])

        def mlp(li, mstack):
            hT = sb("hT", [128, 32, 1024], BF16, mstack)
            rtmp = sb("rtmp", [128, 2, 512], F32, mstack)
            ev = 0
            for tp in range(2):
                tok0 = tp * 1024
                for nb in range(8):
                    wsl, wkey = WS.need(("w1", li, tp, nb))
                    wv = wsl[:, 0:4096].rearrange("p (k n) -> p k n", k=8)
                    for c in range(4):
                        ffc = nb * 4 + c
                        for tg in range(2):
                            bank = 4 + (ev % 2)
                            reg = ps[:, bank * 512:(bank + 1) * 512]
                            for kc in range(KC):
                                mm(reg, wv[:, kc, c * 128:(c + 1) * 128], xT[:, kc, tok0 + tg * 512: tok0 + (tg + 1) * 512],
                                   kc == 0, kc == KC - 1,
                                   r=[wkey] + [("xT", tp * 8 + tg * 4 + i) for i in range(4)], w=[("ps", bank)])
                            rt = rtmp[:, ev % 2, :]
                            S.op("dve", lambda e: e.tensor_scalar(out=rt, in0=reg, scalar1=0.0, scalar2=None, op0=ALU.max),
                                 r=[("ps", bank)], w=[("rtmp", ev % 2)])
                            S.op("act", lambda e: e.activation(out=hT[:, ffc, tg * 512:(tg + 1) * 512], in_=rt, func=AF.Square),
                                 r=[("rtmp", ev % 2)], w=[("hT", ffc, tg)])
                            ev += 1
                for nh in range(2):
                    for tgp in range(2):
                        for g in range(4):
                            wsl, wkey = WS.need(("w2", li, tp, nh, tgp, g))
                            wv = wsl[:, 0:4096].rearrange("p (k n) -> p k n", k=8)
                            for ti in range(4):
                                tl = tgp * 4 + ti
                                reg = ps[:, ti * 512:(ti + 1) * 512]
                                for c in range(8):
                                    ffc = g * 8 + c
                                    mm(reg, hT[:, ffc, tl * 128:(tl + 1) * 128], wv[:, c, :],
                                       g == 0 and c == 0, g == 3 and c == 7,
                                       r=[wkey, ("hT", ffc, tl // 4)], w=[("ps", ti)])
                        for ti in range(4):
                            t = tp * 8 + tgp * 4 + ti
                            reg = ps[:, ti * 512:(ti + 1) * 512]
                            xs = x[:, t, nh * 512:(nh + 1) * 512]
                            S.op("dve", lambda e: e.scalar_tensor_tensor(out=xs, in0=xs, scalar=ALPHA, in1=reg,
                                                                         op0=ALU.mult, op1=ALU.add),
                                 r=[("ps", ti), ("x", t)], w=[("x", t)])
                for tl in range(8):
                    t = tp * 8 + tl
                    layernorm(t, 1)
                    transposes(t)


        def add_residual(t, h, regy):
            for nh in range(2):
                xs = x[:, t, nh * 512:(nh + 1) * 512]
                rg_ = regy[:, nh * 512:(nh + 1) * 512]
                if h == 0:
                    S.op("dve", lambda e: e.scalar_tensor_tensor(out=xs, in0=xs, scalar=ALPHA, in1=rg_, op0=ALU.mult, op1=ALU.add),
                         r=[("ps", nh), ("x", t)], w=[("x", t)])
                else:
                    S.op("dve", lambda e: e.tensor_tensor(out=xs, in0=xs, in1=rg_, op=ALU.add),
                         r=[("ps", nh), ("x", t)], w=[("x", t)])

        def gla(li, ms):
            j = li // 2
            glT = sb("glT", [49, SEQ], BF16, ms)
            gw2b = sb("gw2b", [49, 512], BF16, ms)
            ngbc = sb("ngbc", [128, 256], F32, ms)
            qk = sb("qk", [128, NT, 256], BF16, ms)
            vb = sb("vb", [128, NT, 256], BF16, ms)
            rg = sb("rg", [128, NT, 256], BF16, ms)
            QT = sb("QT", [128, 2, NT, 128], BF16, ms)
            KTt = sb("KTt", [128, 2, 128], BF16, ms)
            scT = sb("scT", [128, 2, NT, 128], BF16, ms)
            kp = sb("kp", [128, 2, NT, 128], BF16, ms)
            Sbs = sb("Sbs", [128, NT, 256], BF16, ms)
            ebl = sb("ebl", [128, 2, NT], F32, ms)
            Sf = sb("Sf", [128, 256], F32, ms)
            Sfb = sb("Sfb", [128, 2, 256], BF16, ms)
            et = sb("et", [128, 1, 128], F32, ms)
            spt = sb("spt", [128, 1, 128], F32, ms)
            e3 = sb("e3", [128, 2, 3, 128], F32, ms)
            qt2 = sb("qt2", [128, 2, 256], BF16, ms)
            rtm = sb("rtm", [128, 1, 256], F32, ms)
            ssq = sb("ssq", [128, NT, 3], F32, ms)
            og = sb("og", [128, 2, 256], BF16, ms)
            ogT = sb("ogT", [128, 2, 2, 128], BF16, ms)
            c_g = S.chan("glap%d" % li)
            S.dma("pool", c_g, gw2b[0:16, :], b_gw2[j][0], w=["gw2b"])
            S.dma("pool", c_g, gw2b[32:48, :], b_gw2[j][1], w=["gw2b"])
            S.dma("pool", c_g, gw2b[16:17, :], b_gb[j][0], w=["gw2b"])
            S.dma("pool", c_g, gw2b[48:49, :], b_gb[j][1], w=["gw2b"])
            S.dma("sp", c_g, ngbc[:], b_ng[j].partition_broadcast(128), w=["ngbc"])
            psb = ps[:, 2560:3072].bitcast(BF16)
            wsl, wkey = WS.need(("gl", li))
            wv = wsl[:, 0:512].rearrange("p (k n) -> p k n", k=8)
            S.op("dve", lambda e: e.memset(wv[:, :, 16:32], 0.0), r=[wkey], w=[wkey])
            for tg in range(4):
                reg = ps[0:48, 1024:1536]
                for kc in range(KC):
                    mm(reg, wv[:, kc, 0:48], xT[:, kc, tg * 512:(tg + 1) * 512], kc == 0, kc == KC - 1,
                       r=[wkey] + [("xT", tg * 4 + i) for i in range(4)], w=[("ps", 2)])
                S.op("act", lambda e: e.activation(out=glT[0:48, tg * 512:(tg + 1) * 512], in_=reg, func=AF.Copy),
                     r=[("ps", 2)], w=[("glT", tg)])
                for rr in (16, 48):
                    S.dma("sp", c_g, glT[rr:rr + 1, tg * 512:(tg + 1) * 512], onesb[0:1, :], r=["onesb"], w=[("glT", tg)])
            import os
            STOP = int(os.environ.get("GLA_STOP", "99"))
            if STOP <= 1:
                return
            for h in range(4):
                if STOP <= 5 and h > 0:
                    return
                wA, kA = WS.need(("glaA", li, h))
                wB, kB = WS.need(("glaB", li, h))
                wAv = wA[:, 0:4096].rearrange("p (k n) -> p k n", k=8)
                wBv = wB[:, 0:2048].rearrange("p (k n) -> p k n", k=8)
                for t in range(NT):
                    reg = ps[:, 1024:1536]
                    for kc in range(KC):
                        mm(reg, xT[:, kc, t * 128:(t + 1) * 128], wAv[:, kc, :], kc == 0, kc == KC - 1,
                           r=[kA, ("xT", t)], w=[("ps", 2)])
                    SK = os.environ.get("GLA_SKIP", "")
                    if "f" not in SK:
                        S.op("act", lambda e: e.activation(out=qk[:, t, :], in_=reg[:, 0:256], func=AF.Copy),
                             r=[("ps", 2)], w=[("qk", t)])
                    if "e" not in SK:
                        S.op("dve", lambda e: e.tensor_copy(out=vb[:, t, :], in_=reg[:, 256:512]), r=[("ps", 2)], w=[("vb", t)])
                    if "B" in SK:
                        continue
                    reg3 = ps[:, 1536:1792]
                    for kc in range(KC):
                        mm(reg3, xT[:, kc, t * 128:(t + 1) * 128], wBv[:, kc, :], kc == 0, kc == KC - 1,
                           r=[kB, ("xT", t)], w=[("ps", 3)])
                    if "c" not in SK:
                        S.op("act", lambda e: e.activation(out=rtm[:, 0, :], in_=reg3, func=AF.Silu),
                             r=[("ps", 3)], w=[("rtm", 0)])
                    if "d" not in SK:
                        S.op("dve", lambda e: e.tensor_tensor(out=rg[:, t, :], in0=rtm[:, 0, :], in1=ngbc[:], op=ALU.mult),
                             r=[("rtm", 0), "ngbc"], w=[("rg", t)])
                if STOP <= 2:
                    return
                it = 0
                for d_ in range(2):
                    for t in range(NT):
                        b_ = it % 2
                        it += 1
                        pg = ps[:, 3072:3200]
                        pc = ps[:, 2176:2432]
                        pcl = ps[:, 2432:2433]
                        mm(pg, glT[32 * d_:32 * d_ + 17, t * 128:(t + 1) * 128], gw2b[32 * d_:32 * d_ + 17, h * 128:(h + 1) * 128], True, True,
                           r=[("glT", t // 4), "gw2b"], w=[("ps", 6)])
                        S.op("act", lambda e: e.activation(out=et[:, 0, :], in_=pg, func=AF.Exp, scale=-1.0),
                             r=[("ps", 6)], w=[("et", 0)])
                        S.op("act", lambda e: e.activation(out=spt[:, 0, :], in_=et[:, 0, :], func=AF.Ln, bias=1.0, scale=1.0),
                             r=[("et", 0)], w=[("spt", 0)])
                        PST = int(os.environ.get("GLA_P", "9"))
                        if PST <= 1:
                            continue
                        mm(pc[:, 0:128], tri[:, 2 * d_, :], spt[:, 0, :], True, True, r=["tri", ("spt", 0)], w=[("ps", 4)])
                        mm(pc[:, 128:256], tri[:, 2 * d_ + 1, :], spt[:, 0, :], True, True, r=["tri", ("spt", 0)], w=[("ps", 4)])
                        mm(pcl, spt[:, 0, :], ones_f[:, 0:1], True, True, r=["ones_f", ("spt", 0)], w=[("ps", 4)])
                        S.op("act", lambda e: e.activation(out=e3[:, b_, 0, :], in_=pc[:, 0:128], func=AF.Exp, scale=-1.0 / 16),
                             r=[("ps", 4)], w=[("e3", b_, 0)])
                        S.op("act", lambda e: e.activation(out=e3[:, b_, 1, :], in_=pc[:, 0:128], func=AF.Exp, scale=1.0 / 16),
                             r=[("ps", 4)], w=[("e3", b_, 1)])
                        S.op("act", lambda e: e.activation(out=e3[:, b_, 2, :], in_=pc[:, 128:256], func=AF.Exp, scale=1.0 / 16),
                             r=[("ps", 4)], w=[("e3", b_, 2)])
                        S.op("act", lambda e: e.activation(out=ebl[:, d_, t:t + 1], in_=pcl, func=AF.Exp, scale=-1.0 / 16),
                             r=[("ps", 4)], w=[("ebl", d_, t)])
                        if PST <= 2:
                            continue
                        S.op("dve", lambda e: e.scalar_tensor_tensor(out=qt2[:, b_, 0:128], in0=qk[:, t, 0:128], scalar=float(128 ** -0.5),
                                                                     in1=e3[:, b_, 0, :], op0=ALU.mult, op1=ALU.mult),
                             r=[("qk", t), ("e3", b_, 0)], w=[("qt2", b_)])
                        S.op("dve", lambda e: e.tensor_tensor(out=qt2[:, b_, 128:256], in0=qk[:, t, 128:256], in1=e3[:, b_, 1, :], op=ALU.mult),
                             r=[("qk", t), ("e3", b_, 1)], w=[("qt2", b_)])
                        S.op("dve", lambda e: e.tensor_tensor(out=kp[:, d_, t, :], in0=qk[:, t, 128:256], in1=e3[:, b_, 2, :], op=ALU.mult),
                             r=[("qk", t), ("e3", b_, 2)], w=[("kp", d_, t)])
                        if PST <= 3:
                            continue
                        for c in range(2):
                            S.op("pe", lambda e: e.transpose(psb[:, c * 128:(c + 1) * 128], qt2[:, b_, c * 128:(c + 1) * 128], identb[:]),
                                 r=[("qt2", b_), "identb"], w=[("ps", 5)])
                        S.op("act", lambda e: e.activation(out=QT[:, d_, t, :], in_=psb[:, 0:128], func=AF.Copy),
                             r=[("ps", 5)], w=[("QT", d_, t)])
                        S.op("act", lambda e: e.activation(out=KTt[:, b_, :], in_=psb[:, 128:256], func=AF.Copy),
                             r=[("ps", 5)], w=[("KTt", b_)])
                        if PST <= 4:
                            continue
                        psc = ps[:, 3584:3712]
                        mm(psc, KTt[:, b_, :], QT[:, d_, t, :], True, True, r=[("QT", d_, t), ("KTt", b_)], w=[("ps", 7)])
                        S.op("dve", lambda e: e.tensor_tensor(out=scT[:, d_, t, :], in0=psc, in1=maskT[:, d_, :], op=ALU.mult),
                             r=[("ps", 7), "maskT"], w=[("scT", d_, t)])
                if STOP <= 3:
                    return
                pS = ps[:, 1024:1280]
                S.op("dve", lambda e: e.memset(Sf[:], 0.0), w=["Sf"])
                for t in range(NT - 1, -1, -1):
                    S.op("act", lambda e: e.activation(out=Sbs[:, t, :], in_=Sf[:], func=AF.Copy), r=["Sf"], w=[("Sbs", t)])
                    mm(pS, kp[:, 1, t, :], vb[:, t, :], True, True, r=[("kp", 1, t), ("vb", t)], w=[("ps", 2)])
                    S.op("dve", lambda e: e.scalar_tensor_tensor(out=Sf[:], in0=Sf[:], scalar=ebl[:, 1, t:t + 1], in1=pS,
                                                                 op0=ALU.mult, op1=ALU.add),
                         r=["Sf", ("ebl", 1, t), ("ps", 2)], w=["Sf"])
                if STOP <= 4:
                    return
                wO, kO = WS.need(("glaO", li, h))
                wOv = wO[:, 0:2048].rearrange("p (k n) -> p k n", k=2)
                S.op("dve", lambda e: e.memset(Sf[:], 0.0), r=["Sf"], w=["Sf"])
                for t in range(NT):
                    b_ = t % 2
                    S.op("act", lambda e: e.activation(out=Sfb[:, b_, :], in_=Sf[:], func=AF.Copy), r=["Sf"], w=[("Sfb", b_)])
                    po = ps[:, 1536:1792]
                    mm(po, QT[:, 0, t, :], Sfb[:, b_, :], True, False, r=[("QT", 0, t), ("Sfb", b_)], w=[("ps", 3)])
                    mm(po, scT[:, 0, t, :], vb[:, t, :], False, False, r=[("scT", 0, t), ("vb", t)], w=[("ps", 3)])
                    mm(po, QT[:, 1, t, :], Sbs[:, t, :], False, False, r=[("QT", 1, t), ("Sbs", t)], w=[("ps", 3)])
                    mm(po, scT[:, 1, t, :], vb[:, t, :], False, True, r=[("scT", 1, t), ("vb", t)], w=[("ps", 3)])
                    mm(pS, kp[:, 0, t, :], vb[:, t, :], True, True, r=[("kp", 0, t), ("vb", t)], w=[("ps", 2)])
                    S.op("dve", lambda e: e.scalar_tensor_tensor(out=Sf[:], in0=Sf[:], scalar=ebl[:, 0, t:t + 1], in1=pS,
                                                                 op0=ALU.mult, op1=ALU.add),
                         r=["Sf", ("ebl", 0, t), ("ps", 2)], w=["Sf"])
                    S.op("act", lambda e: e.activation(out=e3[:, 0, 0:2, :].rearrange("p a b -> p (a b)"), in_=po, func=AF.Square, accum_out=ssq[:, t, 0:1]),
                         r=[("ps", 3)], w=[("e3", 0, 0), ("e3", 0, 1), ("ssq", t)])
                    S.op("act", lambda e: e.activation(out=ssq[:, t, 1:2], in_=ssq[:, t, 0:1], func=AF.Sqrt, bias=1e-6, scale=1.0 / 256),
                         r=[("ssq", t)], w=[("ssq1", t)])
                    S.op("dve", lambda e: e.reciprocal(out=ssq[:, t, 2:3], in_=ssq[:, t, 1:2]), r=[("ssq1", t)], w=[("ssq2", t)])
                    S.op("dve", lambda e: e.scalar_tensor_tensor(out=og[:, b_, :], in0=po, scalar=ssq[:, t, 2:3], in1=rg[:, t, :],
                                                                 op0=ALU.mult, op1=ALU.mult),
                         r=[("ps", 3), ("ssq2", t), ("rg", t)], w=[("og", b_)])
                    for c in range(2):
                        S.op("pe", lambda e: e.transpose(psb[:, c * 128:(c + 1) * 128], og[:, b_, c * 128:(c + 1) * 128], identb[:]),
                             r=[("og", b_), "identb"], w=[("ps", 5)])
                    S.op("act", lambda e: e.activation(out=ogT[:, b_, :, :], in_=psb[:, 0:256].rearrange("p (a b) -> p a b", a=2), func=AF.Copy),
                         r=[("ps", 5)], w=[("ogT", b_)])
                    regy = ps[:, 0:1024]
                    for nh in range(2):
                        for c in range(2):
                            mm(regy[:, nh * 512:(nh + 1) * 512], ogT[:, b_, c, :], wOv[:, c, nh * 512:(nh + 1) * 512], c == 0, c == 1,
                               r=[kO, ("ogT", b_)], w=[("ps", nh)])
                    add_residual(t, h, regy)
                    if h == 3:
                        layernorm(t, 0)
                        transposes(t)


        def gdn(li, ms):
            j = li // 2
            ba = sb("ba", [128, NT, 32], F32, ms)
            sc = sb("gsc", [128, 9, NT, 16], F32, ms)
            alb = sb("alb", [128, 3, 16], F32, ms)
            maskM = sb("maskM", [128, 2, 256], BF16, ms)
            ngb = sb("ngb", [128, 128], F32, ms)
            uT = sb("uT", [128, 2052], BF16, ms)
            cwn = sb("cwn", [5, 128], F32, ms)
            cw = sb("cw", [128, 5], F32, ms)
            cdiag = sb("cdiag", [128, 5, 128], BF16, ms)
            qkn = sb("qkn", [128, NT, 256], BF16, ms)
            vb = sb("vba", [128, NT, 128], BF16, ms)
            qkT = sb("qkT", [128, NT, 256], BF16, ms)
            zg = sb("zg", [128, NT, 128], BF16, ms)
            oacc = sb("oacc", [128, NT, 128], F32, ms)
            cs = sb("cs", [128, 2, 128], F32, ms)
            sqt = sb("sqt", [128, 128], F32, ms)
            nrm = sb("nrm", [128, NT, 3, 3], F32, ms)
            zt = sb("zt", [128, 2, 128], F32, ms)
            diag2 = sb("diag2", [128, 2, 256], F32, ms)
            Dm = sb("Dm", [128, 2, 2, 128], F32, ms)
            Mb = sb("Mb", [128, 2, 2, 128], F32, ms)
            MTT = sb("MTT", [128, 2, 2, 256], F32, ms)
            TTb = sb("TTb", [128, 2, 128], BF16, ms)
            QKTb = sb("QKTb", [128, 2, 128], BF16, ms)
            kbg = sb("kbg", [128, 2, 128], BF16, ms)
            vbeta = sb("vbeta", [128, 2, 128], BF16, ms)
            WT = sb("WT", [128, 2, 128], BF16, ms)
            Ub = sb("Ub", [128, 2, 128], F32, ms)
            dg = sb("dg", [128, 2, 128], BF16, ms)
            QdT = sb("QdT", [128, 2, 128], BF16, ms)
            Kd = sb("Kd", [128, 2, 128], BF16, ms)
            vnew = sb("vnew", [128, 2, 128], BF16, ms)
            Sst = sb("Sst", [128, 2, 128], F32, ms)
            Sbb = sb("Sbb", [128, 2, 128], BF16, ms)
            ogb = sb("ogb", [128, 128], BF16, ms)
            ogTb = sb("ogTb", [128, 128], BF16, ms)
            fin = sb("fin", [128, NT, 3], F32, ms)
            c_g = S.chan("gdnp%d" % li)
            GST = int(os.environ.get("GDN_STOP", "99"))
            S.dma("sp", c_g, alb[:, 0, :], a_alog[j].partition_broadcast(128), w=["alb0"])
            S.dma("sp", c_g, alb[:, 1, :], a_dtb[j].partition_broadcast(128), w=["alb1"])
            S.dma("sp", c_g, ngb[:], a_ng[j].partition_broadcast(128), w=["ngb"])
            for d_ in range(2):
                S.op("pool", lambda e: e.memset(maskM[:, d_, 0:128], -BIG), w=[("maskM", d_)])
                S.op("pool", lambda e: e.memset(maskM[:, d_, 128:256], BIG), r=[("maskM", d_)], w=[("maskM", d_)])
                sg = 1 if d_ == 0 else -1
                S.op("pool", lambda e: e.affine_select(out=maskM[:, d_, 0:128], in_=maskM[:, d_, 0:128], pattern=[[sg, 128]],
                                                       compare_op=ALU.is_ge, fill=0.0, base=0, channel_multiplier=-sg),
                     r=[("maskM", d_)], w=[("maskM", d_)])
                S.op("pool", lambda e: e.affine_select(out=maskM[:, d_, 128:256], in_=maskM[:, d_, 128:256], pattern=[[-sg, 128]],
                                                       compare_op=ALU.is_ge, fill=0.0, base=-1, channel_multiplier=sg),
                     r=[("maskM", d_)], w=[("maskM", d_)])
            S.op("dve", lambda e: e.memset(uT[:, 0:2], 0.0), w=["uTpad"])
            S.op("dve", lambda e: e.memset(uT[:, 2050:2052], 0.0), r=["uTpad"], w=["uTpad"])
            wsl, wkey = WS.need(("ba", li))
            wv = wsl[:, 0:256].rearrange("p (k n) -> p k n", k=8)
            reg = ps[:, 1024:1536]
            for t in range(NT):
                for kc in range(KC):
                    mm(reg[:, t * 32:(t + 1) * 32], xT[:, kc, t * 128:(t + 1) * 128], wv[:, kc, :], kc == 0, kc == KC - 1,
                       r=[wkey, ("xT", t)], w=[("ps", 2)])
            S.op("act", lambda e: e.activation(out=ba[:].rearrange("p a b -> p (a b)"), in_=reg, func=AF.Copy), r=[("ps", 2)], w=["ba"])
            bc16 = lambda ap: ap.unsqueeze(1).to_broadcast([128, NT, 16])
            S.op("act", lambda e: e.activation(out=sc[:, 0], in_=ba[:, :, 0:16], func=AF.Sigmoid), r=["ba"], w=[("sc", 0)])
            S.op("dve", lambda e: e.tensor_scalar(out=sc[:, 1], in0=sc[:, 0], scalar1=-1.0, scalar2=None, op0=ALU.mult),
                 r=[("sc", 0)], w=[("sc", 1)])
            S.op("act", lambda e: e.activation(out=alb[:, 2, :], in_=alb[:, 0, :], func=AF.Exp), r=["alb0"], w=["alb2"])
            S.op("dve", lambda e: e.tensor_tensor(out=sc[:, 8], in0=ba[:, :, 16:32], in1=bc16(alb[:, 1, :]), op=ALU.add),
                 r=["ba", "alb1"], w=[("sc", 8)])
            S.op("act", lambda e: e.activation(out=sc[:, 8], in_=sc[:, 8], func=AF.Exp), r=[("sc", 8)], w=[("sc", 8)])
            S.op("act", lambda e: e.activation(out=sc[:, 8], in_=sc[:, 8], func=AF.Ln, bias=1.0, scale=1.0), r=[("sc", 8)], w=[("sc", 8)])
            S.op("dve", lambda e: e.tensor_tensor(out=sc[:, 8], in0=sc[:, 8], in1=bc16(alb[:, 2, :]), op=ALU.mult),
                 r=[("sc", 8), "alb2"], w=[("sc", 8)])
            pcs = ps[:, 1024:1280]
            pct = ps[:, 1280:1536]
            mm(pcs[:, 0:128], tri[:, 0, :], sc[:, 8, :, 0:8], True, True, r=["tri", ("sc", 8)], w=[("ps", 2)])
            mm(pcs[:, 128:256], tri[:, 2, :], sc[:, 8, :, 8:16], True, True, r=["tri", ("sc", 8)], w=[("ps", 2)])
            mm(pct, ones_f[:], sc[:, 8].rearrange("p a b -> p (a b)"), True, True, r=["ones_f", ("sc", 8)], w=[("ps", 2)])
            for d_ in range(2):
                S.op("act", lambda e: e.activation(out=sc[:, 2, :, d_ * 8:(d_ + 1) * 8],
                                                   in_=pcs[:, d_ * 128:(d_ + 1) * 128].rearrange("p (a b) -> p a b", a=NT), func=AF.Copy),
                     r=[("ps", 2)], w=[("sc", 2)])
            pct3 = pct.rearrange("p (a b) -> p a b", a=NT)
            S.op("dve", lambda e: e.tensor_scalar(out=sc[:, 3], in0=sc[:, 2], scalar1=-1.0, scalar2=None, op0=ALU.mult),
                 r=[("sc", 2)], w=[("sc", 3)])
            S.op("act", lambda e: e.activation(out=sc[:, 4], in_=sc[:, 2], func=AF.Exp, scale=-1.0), r=[("sc", 2)], w=[("sc", 4)])
            S.op("dve", lambda e: e.tensor_tensor(out=sc[:, 5], in0=sc[:, 2], in1=pct3, op=ALU.subtract),
                 r=[("sc", 2), ("ps", 2)], w=[("sc", 5)])
            S.op("act", lambda e: e.activation(out=sc[:, 5], in_=sc[:, 5], func=AF.Exp), r=[("sc", 5)], w=[("sc", 5)])
            S.op("act", lambda e: e.activation(out=sc[:, 7], in_=pct3, func=AF.Exp, scale=-1.0), r=[("ps", 2)], w=[("sc", 7)])
            S.op("dve", lambda e: e.tensor_tensor(out=sc[:, 6], in0=sc[:, 0], in1=sc[:, 4], op=ALU.mult),
                 r=[("sc", 0), ("sc", 4)], w=[("sc", 6)])
            SCK = [("sc", i) for i in range(8)]
            if GST <= 1:
                return
            psb3 = ps[:, 1536:2048].bitcast(BF16)
            for h in range(8):
                if GST <= 7 and h > 0:
                    return
                wA, kA = WS.need(("gdnA", li, h))
                wAv = wA[:, 0:4096].rearrange("p (k n) -> p k n", k=8)
                for t in range(NT):
                    rz = ps[:, 1536 + (t % 2) * 128:1536 + (t % 2 + 1) * 128]
                    for kc in range(KC):
                        mm(rz, xT[:, kc, t * 128:(t + 1) * 128], wAv[:, kc, 384:512], kc == 0, kc == KC - 1,
                           r=[kA, ("xT", t)], w=[("ps", 3)])
                    S.op("act", lambda e: e.activation(out=zt[:, t % 2, :], in_=rz, func=AF.Silu), r=[("ps", 3)], w=[("zt", t % 2)])
                    S.op("dve", lambda e: e.tensor_tensor(out=zg[:, t, :], in0=zt[:, t % 2, :], in1=ngb[:], op=ALU.mult),
                         r=[("zt", t % 2), "ngb"], w=[("zg", t)])
                for c in range(3):
                    S.dma("sp", c_g, cwn[:], a_conv[j][:, c * 1024 + h * 128:c * 1024 + (h + 1) * 128], w=["cwn"])
                    S.op("pe", lambda e: e.transpose(ps[:, 1536 + 256:1536 + 261], cwn[0:5, :], ident[0:5, 0:5]), r=["cwn", "ident"], w=[("ps", 3)])
                    S.op("act", lambda e: e.activation(out=cw[:], in_=ps[:, 1536 + 256:1536 + 261], func=AF.Copy), r=[("ps", 3)], w=["cw"])
                    for jt in range(5):
                        S.op("dve", lambda e: e.tensor_scalar(out=cdiag[:, jt, :], in0=ident[:], scalar1=cw[:, jt:jt + 1], scalar2=None, op0=ALU.mult),
                             r=["ident", "cw"], w=[("cdiag", jt)])
                    for tg in range(4):
                        reg = ps[:, 1024:1536]
                        for kc in range(KC):
                            mm(reg, wAv[:, kc, c * 128:(c + 1) * 128], xT[:, kc, tg * 512:(tg + 1) * 512], kc == 0, kc == KC - 1,
                               r=[kA] + [("xT", tg * 4 + i) for i in range(4)], w=[("ps", 2)])
                        S.op("act", lambda e: e.activation(out=uT[:, 2 + tg * 512:2 + (tg + 1) * 512], in_=reg, func=AF.Copy),
                             r=[("ps", 2)], w=[("uT", tg)])
                    for t in range(NT):
                        b_ = t % 2
                        rc = ps[:, 1536 + b_ * 128:1536 + (b_ + 1) * 128]
                        ukeys = ["uTpad"] + [("uT", g_) for g_ in {max(0, (t * 128 - 2)) // 512, min(2047, t * 128 + 130) // 512}]
                        for jt in range(5):
                            mm(rc, uT[:, t * 128 + jt:t * 128 + jt + 128], cdiag[:, jt, :], jt == 0, jt == 4,
                               r=ukeys + [("cdiag", jt)], w=[("ps", 3)])
                        if c == 2:
                            S.op("act", lambda e: e.activation(out=vb[:, t, :], in_=rc, func=AF.Silu), r=[("ps", 3)], w=[("vb", t)])
                            continue
                        S.op("act", lambda e: e.activation(out=cs[:, b_, :], in_=rc, func=AF.Silu), r=[("ps", 3)], w=[("cs", b_)])
                        S.op("dve", lambda e: e.tensor_tensor(out=sqt[:], in0=cs[:, b_, :], in1=cs[:, b_, :], op=ALU.mult),
                             r=[("cs", b_)], w=["sqt"])
                        S.op("dve", lambda e: e.tensor_reduce(out=nrm[:, t, c, 0:1], in_=sqt[:], axis=AX.X, op=ALU.add),
                             r=["sqt"], w=[("nrm", t, c)])
                        S.op("act", lambda e: e.activation(out=nrm[:, t, c, 1:2], in_=nrm[:, t, c, 0:1], func=AF.Sqrt, bias=1e-6, scale=1.0),
                             r=[("nrm", t, c)], w=[("nrm1", t, c)])
                        S.op("dve", lambda e: e.reciprocal(out=nrm[:, t, c, 2:3], in_=nrm[:, t, c, 1:2]), r=[("nrm1", t, c)], w=[("nrm2", t, c)])
                        qsc = float(128 ** -0.5) if c == 0 else 1.0
                        S.op("dve", lambda e: e.tensor_scalar(out=qkn[:, t, c * 128:(c + 1) * 128], in0=cs[:, b_, :], scalar1=nrm[:, t, c, 2:3],
                                                              scalar2=qsc, op0=ALU.mult, op1=ALU.mult),
                             r=[("cs", b_), ("nrm2", t, c)], w=[("qkn", t, c)])
                for t in range(NT):
                    S.op("pe", lambda e: e.transpose(psb3[:, 0:128], qkn[:, t, 128:256], identb[:]), r=[("qkn", t, 1), "identb"], w=[("ps", 3)])
                    S.op("pe", lambda e: e.transpose(psb3[:, 128:256], qkn[:, t, 0:128], identb[:]), r=[("qkn", t, 0), "identb"], w=[("ps", 3)])
                    S.op("act", lambda e: e.activation(out=qkT[:, t, :], in_=psb3[:, 0:256], func=AF.Copy), r=[("ps", 3)], w=[("qkT", t)])
                if GST <= 2:
                    return
                wO, kO = WS.need(("gdnO", li, h))
                for d_ in range(2):
                    S.op("dve", lambda e: e.memset(Sst[:, d_, :], 0.0), r=[("Sst", d_)], w=[("Sst", d_)])
                    S.op("dve", lambda e: e.memset(Sbb[:, d_, :], 0.0), r=[("Sbb", d_)], w=[("Sbb", d_)])
                for s in range(NT):
                    tt_ = (s, NT - 1 - s)
                    hd_ = (h, 8 + h)
                    bA = (4, 6)
                    bB = (5, 7)

                    def scal(i, d_):
                        return sc[:, i, tt_[d_], hd_[d_]:hd_[d_] + 1]

                    def pA(d_, a, b):
                        return ps[:, bA[d_] * 512 + a:bA[d_] * 512 + b]

                    def pB(d_, a, b):
                        return ps[:, bB[d_] * 512 + a:bB[d_] * 512 + b]

                    def st1(d_):
                        t = tt_[d_]
                        S.op("dve", lambda e: e.tensor_scalar(out=diag2[:, d_, :].rearrange("p (a b) -> p a b", a=2),
                                                              in0=ident[:].unsqueeze(1).to_broadcast([128, 2, 128]),
                                                              scalar1=scal(2, d_), scalar2=None, op0=ALU.mult),
                             r=["ident"] + SCK, w=[("diag2", d_)])
                        mm(pA(d_, 0, 256), ones_f[:], diag2[:, d_, :], True, False, r=["ones_f", ("diag2", d_)], w=[("ps", bA[d_])])
                        mm(pA(d_, 0, 256), identb[:], maskM[:, d_, :], False, True, r=["identb", ("maskM", d_)], w=[("ps", bA[d_])])
                        S.op("act", lambda e: e.activation(out=Dm[:, d_, 0, :], in_=pA(d_, 0, 128), func=AF.Exp, bias=scal(3, d_), scale=1.0),
                             r=[("ps", bA[d_])] + SCK, w=[("Dm", d_, 0)])
                        S.op("act", lambda e: e.activation(out=Dm[:, d_, 1, :], in_=pA(d_, 128, 256), func=AF.Exp, bias=scal(2, d_), scale=-1.0),
                             r=[("ps", bA[d_])] + SCK, w=[("Dm", d_, 1)])
                        mm(pA(d_, 256, 512), qkT[:, t, 0:128], qkT[:, t, :], True, True, r=[("qkT", t)], w=[("ps", bA[d_])])
                        S.op("dve", lambda e: e.scalar_tensor_tensor(out=Mb[:, d_, 0, :], in0=pA(d_, 256, 384), scalar=scal(1, d_), in1=Dm[:, d_, 0, :],
                                                                     op0=ALU.mult, op1=ALU.mult),
                             r=[("ps", bA[d_]), ("Dm", d_, 0)] + SCK, w=[("Mb", d_, 0)])
                        S.op("dve", lambda e: e.tensor_tensor(out=QKTb[:, d_, :], in0=pA(d_, 384, 512), in1=Dm[:, d_, 1, :], op=ALU.mult),
                             r=[("ps", bA[d_]), ("Dm", d_, 1)], w=[("QKTb", d_)])
                        S.op("pe", lambda e: e.transpose(pB(d_, 384, 512), Mb[:, d_, 0, :], ident[:]), r=[("Mb", d_, 0), "ident"], w=[("ps", bB[d_])])
                        S.op("act", lambda e: e.activation(out=MTT[:, d_, 0, 0:128], in_=pB(d_, 384, 512), func=AF.Copy),
                             r=[("ps", bB[d_])], w=[("MT", d_, 0)])
                        S.op("dve", lambda e: e.tensor_tensor(out=MTT[:, d_, 1, 128:256], in0=MTT[:, d_, 0, 0:128], in1=ident[:], op=ALU.add),
                             r=[("MT", d_, 0), "ident"], w=[("TT", d_, 1)])
                        S.op("dve", lambda e: e.tensor_scalar(out=kbg[:, d_, :], in0=qkn[:, t, 128:256], scalar1=scal(6, d_), scalar2=None, op0=ALU.mult),
                             r=[("qkn", t, 1)] + SCK, w=[("kbg", d_)])
                        S.op("dve", lambda e: e.tensor_scalar(out=vbeta[:, d_, :], in0=vb[:, t, :], scalar1=scal(0, d_), scalar2=None, op0=ALU.mult),
                             r=[("vb", t)] + SCK, w=[("vbeta", d_)])
                        S.op("dve", lambda e: e.tensor_scalar(out=dg[:, d_, :], in0=ident[:], scalar1=scal(4, d_), scalar2=None, op0=ALU.mult),
                             r=["ident"] + SCK, w=[("dg", d_)])
                        S.op("dve", lambda e: e.tensor_scalar(out=Kd[:, d_, :], in0=qkn[:, t, 128:256], scalar1=scal(5, d_), scalar2=None, op0=ALU.mult),
                             r=[("qkn", t, 1)] + SCK, w=[("Kd", d_)])

                    def lvl(k):
                        def f(d_):
                            a, b = k % 2, (k + 1) % 2
                            if k <= 5:
                                mm(pB(d_, 0, 128), MTT[:, d_, a, 0:128], Mb[:, d_, a, :], True, True,
                                   r=[("MT", d_, a), ("Mb", d_, a)], w=[("ps", bB[d_])])
                            if k == 0:
                                mm(pB(d_, 128, 256), Mb[:, d_, a, :], MTT[:, d_, a, 0:128], True, True,
                                   r=[("MT", d_, a), ("Mb", d_, a)], w=[("ps", bB[d_])])
                            elif k <= 4:
                                mm(pB(d_, 128, 384), Mb[:, d_, a, :], MTT[:, d_, a, :], True, True,
                                   r=[("MT", d_, a), ("TT", d_, a), ("Mb", d_, a)], w=[("ps", bB[d_])])
                            else:
                                mm(pB(d_, 256, 384), Mb[:, d_, a, :], MTT[:, d_, a, 128:256], True, True,
                                   r=[("TT", d_, a), ("Mb", d_, a)], w=[("ps", bB[d_])])
                            if k <= 5:
                                S.op("act", lambda e: e.activation(out=Mb[:, d_, b, :], in_=pB(d_, 0, 128), func=AF.Copy),
                                     r=[("ps", bB[d_])], w=[("Mb", d_, b)])
                            if k <= 4:
                                S.op("act", lambda e: e.activation(out=MTT[:, d_, b, 0:128], in_=pB(d_, 128, 256), func=AF.Copy),
                                     r=[("ps", bB[d_])], w=[("MT", d_, b)])
                            if k == 0:
                                S.op("dve", lambda e: e.tensor_copy(out=MTT[:, d_, b, 128:256], in_=MTT[:, d_, b, 128:256]),
                                     r=[("TT", d_, b)], w=[("TT", d_, b)])
                            elif k <= 5:
                                S.op("dve", lambda e: e.tensor_tensor(out=MTT[:, d_, b, 128:256], in0=pB(d_, 256, 384), in1=MTT[:, d_, a, 128:256], op=ALU.add),
                                     r=[("ps", bB[d_]), ("TT", d_, a)], w=[("TT", d_, b)])
                            else:
                                S.op("dve", lambda e: e.tensor_tensor(out=TTb[:, d_, :], in0=pB(d_, 256, 384), in1=MTT[:, d_, a, 128:256], op=ALU.add),
                                     r=[("ps", bB[d_]), ("TT", d_, a)], w=[("TTb", d_)])
                        return f

                    def st3(d_):
                        t = tt_[d_]
                        mm(pA(d_, 0, 128), kbg[:, d_, :], TTb[:, d_, :], True, True, r=[("kbg", d_), ("TTb", d_)], w=[("ps", bA[d_])])
                        mm(pA(d_, 128, 256), TTb[:, d_, :], vbeta[:, d_, :], True, True, r=[("vbeta", d_), ("TTb", d_)], w=[("ps", bA[d_])])
                        mm(pA(d_, 256, 384), qkn[:, t, 0:128], dg[:, d_, :], True, True, r=[("qkn", t, 0), ("dg", d_)], w=[("ps", bA[d_])])
                        S.op("act", lambda e: e.activation(out=WT[:, d_, :], in_=pA(d_, 0, 128), func=AF.Copy), r=[("ps", bA[d_])], w=[("WT", d_)])
                        S.op("act", lambda e: e.activation(out=Ub[:, d_, :], in_=pA(d_, 128, 256), func=AF.Copy), r=[("ps", bA[d_])], w=[("Ub", d_)])
                        S.op("act", lambda e: e.activation(out=QdT[:, d_, :], in_=pA(d_, 256, 384), func=AF.Copy), r=[("ps", bA[d_])], w=[("QdT", d_)])

                    def st4(d_):
                        t = tt_[d_]
                        mm(pB(d_, 0, 128), WT[:, d_, :], Sbb[:, d_, :], True, True, r=[("WT", d_), ("Sbb", d_)], w=[("ps", bB[d_])])
                        S.op("dve", lambda e: e.tensor_tensor(out=vnew[:, d_, :], in0=Ub[:, d_, :], in1=pB(d_, 0, 128), op=ALU.subtract),
                             r=[("Ub", d_), ("ps", bB[d_])], w=[("vnew", d_)])
                        mm(pB(d_, 128, 256), QdT[:, d_, :], Sbb[:, d_, :], True, False, r=[("QdT", d_), ("Sbb", d_)], w=[("ps", bB[d_])])
                        mm(pB(d_, 128, 256), QKTb[:, d_, :], vnew[:, d_, :], False, True, r=[("QKTb", d_), ("vnew", d_)], w=[("ps", bB[d_])])
                        mm(pB(d_, 256, 384), Kd[:, d_, :], vnew[:, d_, :], True, True, r=[("Kd", d_), ("vnew", d_)], w=[("ps", bB[d_])])
                        if s <= 7:
                            S.op("act", lambda e: e.activation(out=oacc[:, t, :], in_=pB(d_, 128, 256), func=AF.Copy), r=[("ps", bB[d_])], w=[("oacc", t)])
                        else:
                            S.op("dve", lambda e: e.tensor_tensor(out=oacc[:, t, :], in0=oacc[:, t, :], in1=pB(d_, 128, 256), op=ALU.add),
                                 r=[("ps", bB[d_]), ("oacc", t)], w=[("oacc", t)])
                        S.op("dve", lambda e: e.scalar_tensor_tensor(out=Sst[:, d_, :], in0=Sst[:, d_, :], scalar=scal(7, d_), in1=pB(d_, 256, 384),
                                                                     op0=ALU.mult, op1=ALU.add),
                             r=[("Sst", d_), ("ps", bB[d_])] + SCK, w=[("Sst", d_)])
                        S.op("act", lambda e: e.activation(out=Sbb[:, d_, :], in_=Sst[:, d_, :], func=AF.Copy), r=[("Sst", d_)], w=[("Sbb", d_)])

                    for stage in [st1] + [lvl(k) for k in range(7)] + [st3, st4]:
                        for d_ in range(2):
                            stage(d_)
                    if s >= 8:
                        for d_ in range(2):
                            t = tt_[d_]
                            S.op("act", lambda e: e.activation(out=sqt[:], in_=oacc[:, t, :], func=AF.Square, accum_out=fin[:, t, 0:1]),
                                 r=[("oacc", t)], w=["sqt", ("fin", t)])
                            S.op("act", lambda e: e.activation(out=fin[:, t, 1:2], in_=fin[:, t, 0:1], func=AF.Sqrt, bias=1e-6, scale=1.0 / 128),
                                 r=[("fin", t)], w=[("fin1", t)])
                            S.op("dve", lambda e: e.reciprocal(out=fin[:, t, 2:3], in_=fin[:, t, 1:2]), r=[("fin1", t)], w=[("fin2", t)])
                            S.op("dve", lambda e: e.scalar_tensor_tensor(out=ogb[:], in0=oacc[:, t, :], scalar=fin[:, t, 2:3], in1=zg[:, t, :],
                                                                         op0=ALU.mult, op1=ALU.mult),
                                 r=[("oacc", t), ("fin2", t), ("zg", t)], w=["ogb"])
                            S.op("pe", lambda e: e.transpose(psb3[:, 512:640], ogb[:], identb[:]), r=["ogb", "identb"], w=[("ps", 3)])
                            S.op("act", lambda e: e.activation(out=ogTb[:], in_=psb3[:, 512:640], func=AF.Copy), r=[("ps", 3)], w=["ogTb"])
                            regy = ps[:, 0:1024]
                            for nh in range(2):
                                mm(regy[:, nh * 512:(nh + 1) * 512], ogTb[:], wO[:, nh * 512:(nh + 1) * 512], True, True,
                                   r=[kO, "ogTb"], w=[("ps", nh)])
                            add_residual(t, h, regy)
                            if h == 7:
                                layernorm(t, 0)
                                transposes(t)

        S.dma("sp", c_io, x[:], x_d.rearrange("(t p) d -> p t d", p=128), w=[("x", t) for t in range(NT)])
        for t in range(NT):
            transposes(t)
        for li in layers:
            load_ln_params(li, 0)
            if mixer:
                with ExitStack() as ms:
                    if li % 2 == 0:
                        gdn(li, ms)
                    else:
                        gla(li, ms)
                    S.barrier()
            else:
                for t in range(NT):
                    S.op("dve", lambda e: e.tensor_scalar(out=x[:, t, :], in0=x[:, t, :], scalar1=ALPHA, scalar2=None, op0=ALU.mult),
                         r=[("x", t)], w=[("x", t)])
                    layernorm(t, 0)
                    transposes(t)
            S.barrier()
            load_ln_params(li, 1)
            with ExitStack() as mstack:
                mlp(li, mstack)
                S.barrier()
        S.dma("sp", c_io, y_d.rearrange("(t p) d -> p t d", p=128), x[:], r=[("x", t) for t in range(NT)], w=["y"])
        S.wait_all("sp", ["y"])
        print("ninst", S.ninst, "nwait", S.nwait)
    nc.used_inputs = set(dt.keys())
    return nc


_SHAPES = {
    "a_alog": (2, 1, 16), "a_dt_bias": (2, 1, 16), "a_norm_g": (2, 1, 128), "b_gate_b": (2, 2, 1, 512),
    "b_norm_g": (2, 1, 256), "ln1_g": (4, 1, D), "ln1_b": (4, 1, D), "ln2_g": (4, 1, D), "ln2_b": (4, 1, D),
}


def make_in_maps(inputs, n_cores=8):
    shared = {}
    for k, v in inputs.items():
        if k == "x":
            continue
        a = np.ascontiguousarray(np.asarray(v, dtype=np.float32))
        if k in _SHAPES:
            a = a.reshape(_SHAPES[k])
        shared[k] = a
    xs = np.asarray(inputs["x"], dtype=np.float32)
    maps = []
    for c in range(n_cores):
        m = dict(shared)
        m["x"] = np.ascontiguousarray(xs[c])
        maps.append(m)
    return maps


FUSED = False


def kernel(**inputs):
    maps = make_in_maps(inputs, 8)
    groups = [(0, 1, 2, 3)] if FUSED else [(0, 1), (2, 3)]
    for g in groups:
        nc = build(layers=g)
        ms = [{k: v for k, v in m.items() if k in nc.used_inputs} for m in maps]
        res = run_bass_kernel_spmd(nc, ms, core_ids=list(range(8)))
        for c in range(8):
            maps[c]["x"] = np.ascontiguousarray(res.results[c]["y"])
    return np.stack([maps[c]["x"] for c in range(8)], axis=0).astype(np.float32)
```
